# Optimizing a Trainium2 kernel written in Bass

```python
import jax, jax.numpy as jnp
from jax import lax
import numpy as np

D_MODEL = 1024
BATCH = 16
SEQ = 2048
DEPTH = 1
DEC_BATCH = 8
DEC_SEQ = 32
PAST_LEN = 4096

CHUNK = 64
D_HEAD = 64
H_A = 6
H_B = 6
H_M = 4
D_A = H_A * D_HEAD
D_B = H_B * D_HEAD
D_M = H_M * D_HEAD
D_MIX = D_A + D_B + D_M
A_LEFT_CHUNKS = 8
A_WINDOW = A_LEFT_CHUNKS * CHUNK
REL_CLIP = 256
H_IDX = 8
D_IDX = 32
TOPK_MAX = 256
N_MEM = 256
Q_BLOCK = 128
ROPE_THETA = 10000.0
EPS = 1e-6
ATTN_SCALE = D_HEAD ** -0.5
COL_SIZES = (D_A, D_A, D_A, D_A, D_B, D_B, D_B, D_B, D_M, D_M, H_IDX * D_IDX, D_IDX, H_IDX)
IN_COLS = sum(COL_SIZES)

kernel_name = 'hymba_chunk_band_dsa_memory_step'


def _rmsnorm(x, g):
    xf = x.astype(jnp.float32)
    y = xf * lax.rsqrt(jnp.mean(xf * xf, axis=-1, keepdims=True) + EPS)
    return (y * g.astype(jnp.float32)).astype(x.dtype)


def _rope(x, pos):
    d = x.shape[-1]
    half = d // 2
    inv_freq = ROPE_THETA ** (-jnp.arange(half, dtype=jnp.float32) * 2.0 / d)
    ang = pos.astype(jnp.float32)[:, None] * inv_freq[None, :]
    cos = jnp.cos(ang)[:, None, :]
    sin = jnp.sin(ang)[:, None, :]
    xf = x.astype(jnp.float32)
    x1, x2 = xf[..., :half], xf[..., half:]
    return jnp.concatenate([x1 * cos - x2 * sin, x2 * cos + x1 * sin], axis=-1).astype(x.dtype)


def _split_cols(z):
    offs = [int(o) for o in np.cumsum(COL_SIZES)[:-1]]
    return jnp.split(z, offs, axis=-1)


def _mixer_inputs(x, g, w_in, pos):
    B, T, _ = x.shape
    z = _rmsnorm(x, g) @ w_in
    qa, ka, va, ga, qb, kb, vb, gb, qm, gm, qi, ki, wi = _split_cols(z)
    hd = lambda t, n: t.reshape(B, T, n, D_HEAD)
    qa, ka, va = hd(qa, H_A), hd(ka, H_A), hd(va, H_A)
    qb, kb, vb = _rope(hd(qb, H_B), pos), _rope(hd(kb, H_B), pos), hd(vb, H_B)
    qm = hd(qm, H_M)
    qi = _rope(qi.reshape(B, T, H_IDX, D_IDX), pos)
    ki = _rope(ki.reshape(B, T, 1, D_IDX), pos)[:, :, 0]
    return qa, ka, va, ga, qb, kb, vb, gb, qm, gm, qi, ki, wi


def _memory_kv(mem, g, w_mem_kv):
    B, N, _ = mem.shape
    mk, mv = jnp.split(_rmsnorm(mem, g) @ w_mem_kv, 2, axis=-1)
    return mk.reshape(B, N, H_M, D_HEAD), mv.reshape(B, N, H_M, D_HEAD)


def _attend(q, k, v, bias):
    s = jnp.einsum('bqhd,bkhd->bhqk', q, k).astype(jnp.float32) * ATTN_SCALE
    if bias is not None:
        s = s + bias
    p = jax.nn.softmax(s, axis=-1).astype(v.dtype)
    return jnp.einsum('bhqk,bkhd->bqhd', p, v)


def _rel_bias(table, q_pos, k_pos):
    d = jnp.clip(q_pos[:, None] - k_pos[None, :], -REL_CLIP, REL_CLIP) + REL_CLIP
    return table[:, d].astype(jnp.float32)


def _band_prompt(q, k, v, table):
    B, S, H, dh = q.shape
    nc = S // CHUNK
    band = A_WINDOW + CHUNK
    pad = ((0, 0), (A_WINDOW, 0), (0, 0), (0, 0))
    k_pad, v_pad = jnp.pad(k, pad), jnp.pad(v, pad)
    bias = _rel_bias(table, A_WINDOW + jnp.arange(CHUNK), jnp.arange(band))[None]
    q_chunks = jnp.moveaxis(q.reshape(B, nc, CHUNK, H, dh), 1, 0)

    def one_chunk(args):
        q_c, c = args
        start = c * CHUNK
        k_c = lax.dynamic_slice_in_dim(k_pad, start, band, axis=1)
        v_c = lax.dynamic_slice_in_dim(v_pad, start, band, axis=1)
        valid = (start + jnp.arange(band)) >= A_WINDOW
        return _attend(q_c, k_c, v_c, jnp.where(valid[None, None, None, :], bias, -jnp.inf))

    o = lax.map(one_chunk, (q_chunks, jnp.arange(nc)))
    return jnp.moveaxis(o, 0, 1).reshape(B, S, H * dh)


def _band_sample(q, k_new, v_new, k_cache, v_cache, table):
    B, T, H, dh = q.shape
    P = k_cache.shape[1]
    k = jnp.concatenate([k_cache, k_new], axis=1)
    v = jnp.concatenate([v_cache, v_new], axis=1)
    bias = _rel_bias(table, P + jnp.arange(T), jnp.arange(P + T))[None]
    return _attend(q, k, v, bias).reshape(B, T, H * dh)


def _indexer_scores(qi, ki, wi):
    dots = jnp.einsum('bqhd,bsd->bqhs', qi, ki).astype(jnp.float32) * (D_IDX ** -0.5)
    w = wi.astype(jnp.float32) * (H_IDX ** -0.5)
    return jnp.einsum('bqh,bqhs->bqs', w, jax.nn.relu(dots))


def _gather_attend(q, k, v, scores, topk):
    top_val, top_idx = lax.top_k(scores, topk)
    gather = jax.vmap(lambda kb, ib: kb[ib])
    k_sel = gather(k, top_idx)
    v_sel = gather(v, top_idx)
    s = jnp.einsum('bqhd,bqkhd->bhqk', q, k_sel).astype(jnp.float32) * ATTN_SCALE
    s = jnp.where(jnp.isfinite(top_val)[:, None], s, -jnp.inf)
    p = jax.nn.softmax(s, axis=-1).astype(v.dtype)
    return jnp.einsum('bhqk,bqkhd->bqhd', p, v_sel)


def _dsa_prompt(q, k, v, qi, ki, wi):
    B, S, H, dh = q.shape
    topk = min(TOPK_MAX, S // 4)
    nb = S // Q_BLOCK
    k_pos = jnp.arange(S)
    to_blocks = lambda t: jnp.moveaxis(t.reshape((B, nb, Q_BLOCK) + t.shape[2:]), 1, 0)

    def one_block(args):
        q_b, qi_b, wi_b, start = args
        q_pos = start + jnp.arange(Q_BLOCK)
        admissible = k_pos[None, :] < (q_pos[:, None] // CHUNK + 1) * CHUNK
        sc = jnp.where(admissible[None], _indexer_scores(qi_b, ki, wi_b), -jnp.inf)
        return _gather_attend(q_b, k, v, sc, topk)

    o = lax.map(one_block, (to_blocks(q), to_blocks(qi), to_blocks(wi), jnp.arange(nb) * Q_BLOCK))
    return jnp.moveaxis(o, 0, 1).reshape(B, S, H * dh)


def _dsa_sample(q, k_new, v_new, qi, ki_new, wi, k_cache, v_cache, ki_cache):
    B, T, H, dh = q.shape
    k = jnp.concatenate([k_cache, k_new], axis=1)
    v = jnp.concatenate([v_cache, v_new], axis=1)
    ki = jnp.concatenate([ki_cache, ki_new], axis=1)
    topk = min(TOPK_MAX, k.shape[1] // 4)
    return _gather_attend(q, k, v, _indexer_scores(qi, ki, wi), topk).reshape(B, T, H * dh)


def _merge(x, oa, ob, om, ga, gb, gm, w_out):
    o = jnp.concatenate([oa * jax.nn.silu(ga), ob * jax.nn.silu(gb), om * jax.nn.silu(gm)], axis=-1)
    return x + o @ w_out


def setup_inputs(seed: int = 0) -> dict:
    key = jax.random.key(seed)
    ks = jax.random.split(key, 18)
    nrm = lambda k, shape, scale=1.0: jax.random.normal(k, shape, jnp.float32) * scale
    a_rows = min(A_WINDOW, PAST_LEN)
    return {
        'x_prompt': nrm(ks[0], (BATCH, SEQ, D_MODEL)),
        'x_sample': nrm(ks[1], (DEC_BATCH, DEC_SEQ, D_MODEL)),
        'mem_prompt': nrm(ks[2], (BATCH, N_MEM, D_MODEL)),
        'cache_a_k': nrm(ks[3], (DEPTH, DEC_BATCH, a_rows, H_A, D_HEAD)),
        'cache_a_v': nrm(ks[4], (DEPTH, DEC_BATCH, a_rows, H_A, D_HEAD)),
        'cache_b_k': nrm(ks[5], (DEPTH, DEC_BATCH, PAST_LEN, H_B, D_HEAD)),
        'cache_b_v': nrm(ks[6], (DEPTH, DEC_BATCH, PAST_LEN, H_B, D_HEAD)),
        'cache_b_kidx': nrm(ks[7], (DEPTH, DEC_BATCH, PAST_LEN, D_IDX)),
        'cache_mem_k': nrm(ks[8], (DEPTH, DEC_BATCH, N_MEM, H_M, D_HEAD)),
        'cache_mem_v': nrm(ks[9], (DEPTH, DEC_BATCH, N_MEM, H_M, D_HEAD)),
        'norm_mix_g': 1.0 + nrm(ks[10], (DEPTH, D_MODEL), 0.02),
        'w_in': nrm(ks[11], (DEPTH, D_MODEL, IN_COLS), D_MODEL ** -0.5),
        'rel_bias_a': nrm(ks[12], (DEPTH, H_A, 2 * REL_CLIP + 1), 0.1),
        'norm_mem_g': 1.0 + nrm(ks[13], (DEPTH, D_MODEL), 0.02),
        'w_mem_kv': nrm(ks[14], (DEPTH, D_MODEL, 2 * D_M), D_MODEL ** -0.5),
        'w_out': nrm(ks[15], (DEPTH, D_MIX, D_MODEL), D_MIX ** -0.5),
        'norm_final_g': 1.0 + nrm(ks[16], (D_MODEL,), 0.02),
    }


def reference(x_prompt, x_sample, mem_prompt, cache_a_k, cache_a_v, cache_b_k, cache_b_v, cache_b_kidx,
              cache_mem_k, cache_mem_v, norm_mix_g, w_in, rel_bias_a, norm_mem_g, w_mem_kv, w_out, norm_final_g):
    S = x_prompt.shape[1]
    T = x_sample.shape[1]
    P = cache_b_k.shape[2]
    pos_p = jnp.arange(S)
    pos_s = P + jnp.arange(T)
    keep = min(A_WINDOW, S)
    xp, xs = x_prompt, x_sample
    akp, avp, bkp, bvp, bip, mkp, mvp = [], [], [], [], [], [], []
    aks, avs, bks, bvs, bis = [], [], [], [], []
    for l in range(DEPTH):
        qa, ka, va, ga, qb, kb, vb, gb, qm, gm, qi, ki, wi = _mixer_inputs(xp, norm_mix_g[l], w_in[l], pos_p)
        mk, mv = _memory_kv(mem_prompt, norm_mem_g[l], w_mem_kv[l])
        oa = _band_prompt(qa, ka, va, rel_bias_a[l])
        ob = _dsa_prompt(qb, kb, vb, qi, ki, wi)
        om = _attend(qm, mk, mv, None).reshape(xp.shape[0], S, D_M)
        xp = _merge(xp, oa, ob, om, ga, gb, gm, w_out[l])
        akp.append(ka[:, S - keep:])
        avp.append(va[:, S - keep:])
        bkp.append(kb)
        bvp.append(vb)
        bip.append(ki)
        mkp.append(mk)
        mvp.append(mv)
        qa, ka, va, ga, qb, kb, vb, gb, qm, gm, qi, ki, wi = _mixer_inputs(xs, norm_mix_g[l], w_in[l], pos_s)
        oa = _band_sample(qa, ka, va, cache_a_k[l], cache_a_v[l], rel_bias_a[l])
        ob = _dsa_sample(qb, kb, vb, qi, ki, wi, cache_b_k[l], cache_b_v[l], cache_b_kidx[l])
        om = _attend(qm, cache_mem_k[l], cache_mem_v[l], None).reshape(xs.shape[0], T, D_M)
        xs = _merge(xs, oa, ob, om, ga, gb, gm, w_out[l])
        aks.append(ka)
        avs.append(va)
        bks.append(kb)
        bvs.append(vb)
        bis.append(ki)
    y_prompt = _rmsnorm(xp, norm_final_g)
    y_sample = _rmsnorm(xs, norm_final_g)
    st = lambda t: jnp.stack(t, axis=0)
    return (y_prompt, y_sample, st(akp), st(avp), st(bkp), st(bvp), st(bip), st(mkp), st(mvp),
            st(aks), st(avs), st(bks), st(bvs), st(bis))
```

```python
from contextlib import ExitStack
import numpy as np
import ml_dtypes
import concourse.bass as bass
import concourse.mybir as mybir
from concourse.bass_utils import run_bass_kernel_spmd

F32 = mybir.dt.float32
BF16 = mybir.dt.bfloat16
ALU = mybir.AluOpType
AF = mybir.ActivationFunctionType
AX = mybir.AxisListType

D = 1024
S = 2048
NT = 16
T = 32
PAST = 4096
INC = 3880
NEG = -30000.0
NBIG = -1.0e30
KIT = 16
NKB = 33
SEGS = [(0, 384), (384, 768), (768, 1152), (1152, 1536), (1536, 1920), (1920, 2304),
        (2304, 2688), (2688, 3072), (3072, 3584), (3584, 3880)]


def acp(e, out, in_):
    return e.mul(out=out, in_=in_, mul=1.0)


class Buf:
    def __init__(self, name="", excl=False):
        self.name = name
        self.lw = None
        self.rd = {}
        self.excl = excl


class Prog:
    NDS = 16

    def __init__(self, nc, st):
        self.nc = nc
        self.E = {"pe": nc.tensor, "act": nc.scalar, "dve": nc.vector, "pool": nc.gpsimd, "sp": nc.sync}
        keys = ["pe", "act", "dve", "pool"] + ["d%d" % i for i in range(self.NDS)] + ["q%d" % i for i in range(self.NDS)]
        self.sem = {k: st.enter_context(nc.semaphore("s_" + k)) for k in keys}
        self.cnt = {k: 0 for k in keys}
        self.waited = {e: {k: 0 for k in keys} for e in self.E}
        self.nins = 0
        self.ndma = 0
        self.nsw = 0

    def _wait(self, eng, key, v):
        if self.waited[eng][key] >= v:
            return
        self.E[eng].wait_ge(self.sem[key], v)
        self.waited[eng][key] = v

    def op(self, eng, fn, reads=(), writes=(), dma=False):
        deps = {}
        for b in reads:
            if b.lw is not None:
                deps[b.lw[0]] = max(deps.get(b.lw[0], 0), b.lw[1])
            if b.excl:
                for e, v in b.rd.items():
                    if e != eng:
                        deps[e] = max(deps.get(e, 0), v)
        for b in writes:
            if b.lw is not None:
                deps[b.lw[0]] = max(deps.get(b.lw[0], 0), b.lw[1])
            for e, v in b.rd.items():
                deps[e] = max(deps.get(e, 0), v)
        for p, v in deps.items():
            if p == eng and eng == "pe":
                continue
            self._wait(eng, p, v)
        if dma:
            if eng == "pool":
                key = "q%d" % (self.nsw % self.NDS)
                self.nsw += 1
            else:
                key = "d%d" % (self.ndma % self.NDS)
                self.ndma += 1
            if self.cnt[key] > 0:
                self._wait(eng, key, self.cnt[key])
            inc = 16
        else:
            key = eng
            inc = 1
        self.cnt[key] += inc
        val = self.cnt[key]
        fn(self.E[eng]).then_inc(self.sem[key], inc)
        self.nins += 1
        for b in reads:
            b.rd[key] = max(b.rd.get(key, 0), val)
        for b in writes:
            b.lw = (key, val)
            b.rd = {}

    def dma_barrier(self, eng):
        for k in self.cnt:
            if (k.startswith("d") or k.startswith("q")) and k not in ("dve",) and self.cnt[k] > 0:
                self._wait(eng, k, self.cnt[k])

    def finish(self):
        for k in self.cnt:
            if self.cnt[k] > 0:
                self._wait("sp", k, self.cnt[k])


DBG = {"nb": 2, "nt": NT, "sample": True, "mem": True, "setup": 99}


def build_nc():
    nc = bass.Bass("TRN2", target_bir_lowering=False)
    dt = nc.dram_tensor

    def din(name, shape, dtype=F32):
        return dt(name, list(shape), dtype, kind="ExternalInput").ap()

    def dout(name, shape):
        return dt(name, list(shape), F32, kind="ExternalOutput").ap()

    xp = din("xp", [2, S, D]); xs = din("xs", [T, D]); memp = din("memp", [2, 256, D])
    ca_k = din("ca_k", [512, 384]); ca_v = din("ca_v", [512, 384])
    cb_k = din("cb_k", [PAST, 384]); cb_v = din("cb_v", [PAST, 384]); cb_ki = din("cb_ki", [PAST, 32])
    cm_k = din("cm_k", [256, 256]); cm_v = din("cm_v", [256, 256])
    g_mix = din("g_mix", [D]); w_in = din("w_in", [D, INC]); g_mem = din("g_mem", [D])
    w_mem = din("w_mem", [D, 512]); w_out = din("w_out", [D, D]); g_fin = din("g_fin", [D])
    c_ident = din("c_ident", [128, 128], BF16); c_dmask = din("c_dmask", [128, 128], BF16)
    c_pow2 = din("c_pow2", [128, KIT + 1]); c_maskA = din("c_maskA", [128, 5 * 128], BF16)
    c_ropeT = din("c_ropeT", [PAST + 128, 96]); c_ropeI = din("c_ropeI", [PAST + 128, 48])
    c_biasP = din("c_biasP", [128, 6 * 5 * 128]); c_biasS = din("c_biasS", [128, 5 * 6 * 32])

    y_p = dout("y_p", [2, S, D]); y_s = dout("y_s", [T, D])
    akp = dout("akp", [2, 512, 384]); avp = dout("avp", [2, 512, 384])
    bkp = dout("bkp", [2, S, 384]); bvp = dout("bvp", [2, S, 384]); bip = dout("bip", [2, S, 32])
    mkp = dout("mkp", [2, 256, 256]); mvp = dout("mvp", [2, 256, 256])
    aks = dout("aks", [T, 384]); avs = dout("avs", [T, 384]); bks = dout("bks", [T, 384])
    bvs = dout("bvs", [T, 384]); bis = dout("bis", [T, 32])

    wbfg = [dt("wbf%d" % g, [128, 8, c1 - c0], BF16).ap() for g, (c0, c1) in enumerate(SEGS)]
    wmbf = dt("wmbf", [128, 8, 512], BF16).ap()

    st = ExitStack()
    with st:
        def sb(name, shape, dtype):
            return st.enter_context(nc.sbuf_tensor(name, list(shape), dtype))

        def ps(name, shape, dtype):
            return st.enter_context(nc.psum_tensor(name, list(shape), dtype))

        P = Prog(nc, st)
        kaT = sb("kaT", [128, 3, S], BF16); va = sb("va", [128, NT, 390], BF16)
        kbT = sb("kbT", [128, 3, NKB * 128], BF16); vb = sb("vb", [128, NKB, 390], BF16)
        kiT = sb("kiT", [32, NKB * 128], BF16)
        mkT = sb("mkT", [128, 2, 256], BF16); mv = sb("mv", [128, 2, 260], BF16)
        Isb = sb("Isb", [128, NKB * 128], F32); Mb = sb("Mb", [128, NKB * 128], BF16)
        MT = sb("MT", [128, NKB, 128], BF16)
        wo = sb("wo", [128, 8, D], BF16)
        wch = [sb("wch%d" % i, [128, 8 * 512], BF16) for i in range(2)]
        xt = sb("xt", [128, D], F32); xn = sb("xn", [128, D], BF16); xnT = sb("xnT", [128, 8, 128], BF16)
        biasP = sb("biasP", [128, 6, 5, 128], BF16); biasS = sb("biasS", [128, 5, 6, 32], BF16)
        maskA = sb("maskA", [128, 5, 128], BF16); ident = sb("ident", [128, 128], BF16)
        dmask = sb("dmask", [128, 128], BF16); gfb = sb("gfb", [128, D], F32)
        xres = sb("xres", [128, D], F32)
        kst = [sb("kst%d" % i, [128, 384], F32) for i in range(2)]
        qtok = sb("qtok", [128, 384], BF16)
        qmtok = qtok
        ktok = qtok
        gates = [sb("gates%d" % i, [128, D], BF16) for i in range(2)]
        oall = [sb("oall%d" % i, [128, D], BF16) for i in range(2)]
        rt1 = sb("rt1", [128, 384], F32); rt2 = sb("rt2", [128, 384], F32)
        ropeT = sb("ropeT", [128, 96], F32); ropeI = sb("ropeI", [128, 48], F32)
        qaT = [sb("qaT%d" % i, [128, 3, 128], BF16) for i in range(2)]; qbT = [sb("qbT%d" % i, [128, 3, 128], BF16) for i in range(2)]
        qmT = [sb("qmT%d" % i, [128, 2, 128], BF16) for i in range(2)]
        qib = sb("qib", [128, 256], BF16); qid = sb("qid", [32, 1024], BF16)
        oT = sb("oT", [128, 8, 128], BF16)
        kis = sb("kis", [128, 32], F32); kib = sb("kib", [128, 32], BF16)
        wis = sb("wis", [128, 8], F32); sgn = sb("sgn", [128, 8], BF16); absw = sb("absw", [128, 8], F32)
        Et = sb("Et", [128, 128], BF16); Z = sb("Z", [128, 8, 128], BF16)
        PT = [sb("PT%d" % i, [128, 640], BF16) for i in range(2)]
        otmp = sb("otmp", [128, 384], F32); rden = sb("rden", [128, 8], F32)
        sm = sb("sm", [128, 64], F32)
        halfs = sb("halfs", [128, KIT + 1], F32); pow2 = sb("pow2", [128, KIT + 1], F32)
        cntb = sb("cntb", [128, KIT + 2], F32)
        gcol = sb("gcol", [128, 16], F32)
        zps = [ps("zps%d" % i, [128, 512], F32) for i in range(2)]
        tps = ps("tps", [128, 1024], BF16)
        sps = ps("sps", [128, 1024], F32)
        ips = ps("ips", [128, 512], F32)
        ops = ps("ops", [128, 512], F32)
        tps2 = ps("tps2", [128, 1024], BF16)

        B = lambda n: Buf(n)
        b_kaT = [B("kaT") for _ in range(NT)]; b_va = [B("va") for _ in range(NT)]
        b_kbT = [B("kbT") for _ in range(NKB)]; b_vb = [B("vb") for _ in range(NKB)]
        b_kiT = [B("kiT") for _ in range(NKB)]
        b_mk = B("mk"); b_mv = B("mv"); b_I = B("I"); b_Mb = B("Mb"); b_MT = B("MT"); b_wo = B("wo")
        b_wch = [B("wch0"), B("wch1")]; b_xt = B("xt"); b_xn = B("xn"); b_xnT = B("xnT")
        b_const = B("const"); b_biasS = B("biasS"); b_xres = B("xres"); b_kst = [B("kst0"), B("kst1")]
        b_qtok = B("qtok"); b_qmtok = b_qtok; b_ktok = b_qtok; b_gates = [B("gates0"), B("gates1")]; b_oall = [B("oall0"), B("oall1")]
        b_rt1 = B("rt1"); b_rt2 = B("rt2"); b_rope = B("rope")
        b_qaT = [B("qaT0"), B("qaT1")]; b_qbT = [B("qbT0"), B("qbT1")]; b_qmT = [B("qmT0"), B("qmT1")]; b_oT = B("oT"); b_qib = B("qib"); b_qid = B("qid")
        b_kis = B("kis"); b_kib = B("kib"); b_wis = B("wis"); b_sgn = B("sgn"); b_absw = B("absw")
        b_Et = B("Et"); b_Z = B("Z"); b_PT = [B("PT0"), B("PT1")]; b_otmp = B("otmp"); b_rden = B("rden")
        b_smn = B("smn"); b_smt = B("smt"); b_smo = B("smo"); b_halfs = B("halfs"); b_cnt = B("cnt")
        BX = lambda n: Buf(n, excl=True)
        b_zps = [BX("zps0"), BX("zps1")]; b_tps = BX("tps"); b_sps = [BX("sps0"), BX("sps1")]; b_ips = BX("ips")
        b_ops = BX("ops"); b_tps2 = BX("tps2")
        b_wbf = B("wbf"); b_wmbf = B("wmbf")
        b_out = B("out")

        dma = lambda fn, r, w: P.op("sp", fn, r, w, dma=True)
        dmo = lambda fn, r, w: P.op("pool", fn, r, w, dma=True)
        dma2 = lambda fn, r, w: P.op("act", fn, r, w, dma=True)
        act = lambda fn, r, w: P.op("act", fn, r, w)
        dve = lambda fn, r, w: P.op("dve", fn, r, w)
        pool = lambda fn, r, w: P.op("pool", fn, r, w)
        pe = lambda fn, r, w: P.op("pe", fn, r, w)

        dma(lambda e: e.dma_start(out=ident[:], in_=c_ident[:, :]), [], [b_const])
        dma(lambda e: e.dma_start(out=dmask[:], in_=c_dmask[:, :]), [], [b_const])
        dma(lambda e: e.dma_start(out=pow2[:], in_=c_pow2[:, :]), [], [b_const])
        dma(lambda e: e.dma_start(out=maskA[:].rearrange("p a b -> p (a b)"), in_=c_maskA[:, :]), [], [b_const])
        dma(lambda e: e.dma_start(out=gfb[:], in_=g_fin.partition_broadcast(128)), [], [b_const])
        dma(lambda e: e.dma_start(out=gcol[:, 0:8], in_=g_mix.rearrange("(k p) -> p k", p=128),
                                  allow_slow_non_contiguous=True), [], [b_const])
        dma(lambda e: e.dma_start(out=gcol[:, 8:16], in_=g_mem.rearrange("(k p) -> p k", p=128),
                                  allow_slow_non_contiguous=True), [], [b_const])
        if DBG["setup"] <= 1:
            P.finish()
            return nc
        pool(lambda e: e.memset(va[:].rearrange("p a b -> p (a b)"), 1.0), [], b_va)
        pool(lambda e: e.memset(vb[:].rearrange("p a b -> p (a b)"), 1.0), [], b_vb)
        pool(lambda e: e.memset(mv[:].rearrange("p a b -> p (a b)"), 1.0), [], [b_mv])
        pool(lambda e: e.memset(Z[:].rearrange("p a b -> p (a b)"), 0.0), [], [b_Z])

        NPC = 5
        b_Ih = [B("Ih%d" % i) for i in range(NPC)]; b_Mh = [B("Mh%d" % i) for i in range(NPC)]
        dma(lambda e: e.dma_start(out=Isb[:, 0:3840], in_=c_biasP[:, :]), [], list(b_Ih))
        dve(lambda e: e.tensor_copy(out=biasP[:].rearrange("p a b c -> p (a b c)"), in_=Isb[:, 0:3840]), list(b_Ih), [b_const])
        for h in range(6):
            dve(lambda e: e.tensor_tensor(out=biasP[:, h, :, :], in0=biasP[:, h, :, :], in1=maskA[:, :, :], op=ALU.add),
                [b_const], [b_const])
        dma(lambda e: e.dma_start(out=Isb[:, 0:960], in_=c_biasS[:, :]), [], list(b_Ih))
        dve(lambda e: e.tensor_copy(out=biasS[:].rearrange("p a b c -> p (a b c)"), in_=Isb[:, 0:960]), list(b_Ih), [b_const, b_biasS])
        if DBG["setup"] <= 2:
            P.finish()
            return nc
        def cache_tile(src_k, src_v, j, kT, b_kTj, V, b_Vj, ncols, nh, src_ki=None):
            s_ = j % 2
            dmo(lambda e: e.dma_start(out=kst[s_][:, 0:ncols], in_=src_k[j * 128:(j + 1) * 128, :]), [], [b_kst[s_]])
            act(lambda e: acp(e, out=qtok[:, 0:ncols], in_=kst[s_][:, 0:ncols]), [b_kst[s_]], [b_qtok])
            nch = ncols // 128
            for c in range(nch):
                pe(lambda e: e.transpose(out=tps[:, c * 128:(c + 1) * 128], in_=qtok[:, c * 128:(c + 1) * 128],
                                         identity=ident[:, :]), [b_qtok, b_const], [b_tps])
            act(lambda e: acp(e, out=kT[:, 0:nch, j * 128:(j + 1) * 128],
                              in_=tps[:, 0:nch * 128].rearrange("p (a b) -> p a b", b=128)), [b_tps], [b_kTj])
            dmo(lambda e: e.dma_start(out=kst[s_][:, 0:ncols], in_=src_v[j * 128:(j + 1) * 128, :]), [], [b_kst[s_]])
            dve(lambda e: e.tensor_copy(out=V[:, j, 0:nh * 65].rearrange("p (h d) -> p h d", d=65)[:, :, 0:64],
                                        in_=kst[s_][:, 0:ncols].rearrange("p (h d) -> p h d", d=64)), [b_kst[s_]], [b_Vj])
            if src_ki is not None:
                dmo(lambda e: e.dma_start(out=kis[:, :], in_=src_ki[j * 128:(j + 1) * 128, :]), [], [b_kis])
                act(lambda e: acp(e, out=kib[:, :], in_=kis[:, :]), [b_kis], [b_kib])
                pe(lambda e: e.transpose(out=tps2[0:32, 0:128], in_=kib[:, 0:32], identity=ident[:, :]), [b_kib, b_const], [b_tps2])
                act(lambda e: acp(e, out=kiT[0:32, j * 128:(j + 1) * 128], in_=tps2[0:32, 0:128]), [b_tps2], [b_kiT[j]])

        def CACHE_ALL():
            for j in range(4):
                cache_tile(ca_k, ca_v, j, kaT, b_kaT[j], va, b_va[j], 384, 6)
                yield
            for j in range(2):
                cache_tile(cm_k, cm_v, j, mkT, b_mk, mv, b_mv, 256, 4)
                yield
            for j in range(32):
                cache_tile(cb_k, cb_v, j, kbT, b_kbT[j], vb, b_vb[j], 384, 6, src_ki=cb_ki)
                yield

        cache_gen = CACHE_ALL() if DBG["sample"] else iter(())

        def cache_steps(n):
            for _ in range(n):
                next(cache_gen, None)

        halves = [(0, 768), (768, 1536), (1536, 2304), (2304, 3072), (3072, INC)]
        for k in range(8):
            for hf, (h0, h1) in enumerate(halves):
                dma(lambda e: e.dma_start(out=Isb[:, h0:h1], in_=w_in[k * 128:(k + 1) * 128, h0:h1]), [], [b_Ih[hf]])
                dve(lambda e: e.tensor_scalar(out=Mb[:, h0:h1], in0=Isb[:, h0:h1], scalar1=gcol[:, k:k + 1], scalar2=None,
                                              op0=ALU.mult), [b_Ih[hf], b_const], [b_Mh[hf]])
                for g, (c0, c1) in enumerate(SEGS):
                    if c0 >= h0 and c1 <= h1:
                        dma2(lambda e: e.dma_start(out=wbfg[g][:, k, :], in_=Mb[:, c0:c1]), [b_Mh[hf]], [b_wbf])
                cache_steps(1)
            dma(lambda e: e.dma_start(out=xt[:, 0:512], in_=w_mem[k * 128:(k + 1) * 128, :]), [], [b_xt])
            dve(lambda e: e.tensor_scalar(out=xn[:, 0:512], in0=xt[:, 0:512], scalar1=gcol[:, 8 + k:9 + k], scalar2=None,
                                          op0=ALU.mult), [b_xt, b_const], [b_xn])
            dma2(lambda e: e.dma_start(out=wmbf[:, k, :], in_=xn[:, 0:512]), [b_xn], [b_wmbf])
            dma(lambda e: e.dma_start(out=xres[:, :], in_=w_out[k * 128:(k + 1) * 128, :]), [], [b_xres])
            dve(lambda e: e.tensor_copy(out=wo[:, k, :], in_=xres[:, :]), [b_xres], [b_wo])
        cache_steps(64)
        dve(lambda e: e.memset(cntb[:, 0:1], 0.0), list(b_Ih) + list(b_Mh), [b_I, b_Mb, b_cnt])
        if DBG["setup"] <= 4:
            P.finish()
            return nc
        if DBG["setup"] <= 5:
            P.finish()
            return nc
        for _e in ("sp", "pe", "act", "dve", "pool"):
            P.dma_barrier(_e)
        wslot = [0]
        bg = []

        def tick():
            for g in list(bg):
                try:
                    next(g)
                except StopIteration:
                    bg.remove(g)

        def drain_bg():
            while bg:
                tick()

        def rmsnorm_T(nq, src_ap, split=False):
            dma(lambda e: e.dma_start(out=xt[0:nq, :], in_=src_ap), [], [b_xt])
            act(lambda e: e.activation(out=xn[0:nq, :], in_=xt[0:nq, :], func=AF.Square,
                                       accum_out=sm[0:nq, 0:1]), [b_xt], [b_xn, b_smn])
            dve(lambda e: e.tensor_scalar(out=sm[0:nq, 1:2], in0=sm[0:nq, 0:1], scalar1=1.0 / D, scalar2=1e-6,
                                          op0=ALU.mult, op1=ALU.add), [b_smn], [b_smn])
            act(lambda e: e.activation(out=sm[0:nq, 2:3], in_=sm[0:nq, 1:2], func=AF.Sqrt), [b_smn], [b_smn])
            dve(lambda e: e.reciprocal(out=sm[0:nq, 3:4], in_=sm[0:nq, 2:3]), [b_smn], [b_smn])
            dve(lambda e: e.tensor_scalar(out=xn[0:nq, :], in0=xt[0:nq, :], scalar1=sm[0:nq, 3:4], scalar2=None,
                                          op0=ALU.mult), [b_xt, b_smn], [b_xn])
            if split:
                return
            rmsnorm_T2(nq)

        def rmsnorm_T2(nq):
            for c in range(8):
                pe(lambda e: e.transpose(out=tps[:, c * 128:c * 128 + nq], in_=xn[0:nq, c * 128:(c + 1) * 128],
                                         identity=ident[0:nq, 0:nq]), [b_xn, b_const], [b_tps])
            act(lambda e: acp(e, out=xnT[:, :, 0:nq], in_=tps[:].rearrange("p (a b) -> p a b", b=128)[:, :, 0:nq]),
                [b_tps], [b_xnT])

        loaded = [None, None]

        def wload(s_, key, wsrc, b_wsrc, w):
            if loaded[s_] == key:
                return
            wv = wch[s_][:, 0:8 * w].rearrange("p (k w) -> p k w", w=w)
            dma(lambda e: e.dma_start(out=wv, in_=wsrc), [b_wsrc], [b_wch[s_]])
            loaded[s_] = key

        def project(nq, wsrc, b_wsrc, w, g=None):
            if g is None:
                s = wslot[0] % 2
                wslot[0] += 1
                wload(s, ("m", id(wsrc)), wsrc, b_wsrc, w)
            else:
                s = g % 2
                wload(s, ("g", g), wsrc, b_wsrc, w)
                g2 = (g + 1) % len(SEGS)
                wload(g2 % 2, ("g", g2), wbfg[g2], b_wbf, SEGS[g2][1] - SEGS[g2][0])
            wv = wch[s][:, 0:8 * w].rearrange("p (k w) -> p k w", w=w)
            for k in range(8):
                pe(lambda e: e.matmul(zps[s][0:nq, 0:w], lhsT=xnT[:, k, 0:nq], rhs=wv[:, k, :],
                                      start=(k == 0), stop=(k == 7)), [b_xnT, b_wch[s]], [b_zps[s]])
            flush_deferred()
            return zps[s], b_zps[s]

        def rope(nq, z, bz, H, half, tab, out_ap, b_o):
            w = H * 2 * half
            zv = z[0:nq, 0:w].rearrange("p (h t d) -> p h t d", h=H, t=2)
            cosb = tab[0:nq, 0:half].unsqueeze(1).unsqueeze(1).to_broadcast([nq, H, 2, half])
            sinb = tab[0:nq, half:2 * half].unsqueeze(1).to_broadcast([nq, H, half])
            nsinb = tab[0:nq, 2 * half:3 * half].unsqueeze(1).to_broadcast([nq, H, half])
            t1v = rt1[0:nq, 0:w].rearrange("p (h t d) -> p h t d", h=H, t=2)
            t2v = rt2[0:nq, 0:w].rearrange("p (h t d) -> p h t d", h=H, t=2)
            dve(lambda e: e.tensor_tensor(out=t1v, in0=zv, in1=cosb, op=ALU.mult), [bz, b_rope], [b_rt1])
            dve(lambda e: e.tensor_tensor(out=t2v[:, :, 0, :], in0=zv[:, :, 1, :], in1=nsinb, op=ALU.mult), [bz, b_rope], [b_rt2])
            dve(lambda e: e.tensor_tensor(out=t2v[:, :, 1, :], in0=zv[:, :, 0, :], in1=sinb, op=ALU.mult), [bz, b_rope], [b_rt2])
            wr = [b_o] if b_o is not b_rt1 else [b_rt1]
            dve(lambda e: e.tensor_tensor(out=out_ap, in0=rt1[0:nq, 0:w], in1=rt2[0:nq, 0:w], op=ALU.add),
                [b_rt1, b_rt2], wr)

        deferred = []
        later = []

        def run_later():
            while later:
                later.pop(0)()


        def flush_deferred():
            cur = deferred[:]
            del deferred[:]
            for f in cur:
                f()

        def transpose_to(nq, src, b_src, ncols, dst_fn, b_dst, pt=tps, b_pt=None, defer=False):
            if defer:
                deferred.append(lambda: transpose_to(nq, src, b_src, ncols, dst_fn, b_dst, pt, b_pt))
                return
            b_pt = b_pt or b_tps
            nch = ncols // 128
            for c in range(nch):
                pe(lambda e: e.transpose(out=pt[:, c * 128:c * 128 + nq], in_=src[0:nq, c * 128:(c + 1) * 128],
                                         identity=ident[0:nq, 0:nq]), [b_src, b_const], [b_pt])
            for c in range(nch):
                act(lambda e: acp(e, out=dst_fn(c), in_=pt[:, c * 128:c * 128 + nq]), [b_pt], [b_dst])

        def attend(nq, par, nheads, qT, b_q, kT, b_k, V, b_v, tiles, bias_fn, mask_fn, use_mt, ocol0, blockbias=None):
            nt = len(tiles)
            nblk = (nt + 3) // 4
            units = [(bi, h) for bi in range(nblk) for h in range(nheads)]

            def scores(ui):
                bi, h = units[ui]
                blk = tiles[bi * 4:(bi + 1) * 4]
                c, pb = h // 2, (h % 2) * 64
                sp = ui % NSL
                spt = slot_ps[sp]
                for j, (kt, nk) in enumerate(blk):
                    last = bias_fn is None
                    if blockbias is not None:
                        pe(lambda e: e.matmul(spt[0:nk, j * 128:j * 128 + nq],
                                              lhsT=kT[pb:pb + 64, c, kt * 128:kt * 128 + nk],
                                              rhs=qT[pb:pb + 64, c, 0:nq], start=(j == 0), stop=False),
                           [b_k[kt], b_q], [slot_b[sp]])
                        continue
                    pe(lambda e: e.matmul(spt[0:nk, j * 128:j * 128 + nq],
                                          lhsT=kT[pb:pb + 64, c, kt * 128:kt * 128 + nk],
                                          rhs=qT[pb:pb + 64, c, 0:nq], start=True, stop=last),
                       [b_k[kt], b_q], [slot_b[sp]])
                    if bias_fn is not None:
                        m_ap = mask_fn(bi * 4 + j, nk, nq) if mask_fn is not None else None
                        pe(lambda e: e.matmul(spt[0:nk, j * 128:j * 128 + nq], lhsT=ident[0:nk, 0:nk],
                                              rhs=bias_fn(bi * 4 + j, h, nk, nq), start=False, stop=(m_ap is None)),
                           [b_const] + ([b_biasS] if nq < 128 else []), [slot_b[sp]])
                        if m_ap is not None:
                            pe(lambda e: e.matmul(spt[0:nk, j * 128:j * 128 + nq], lhsT=ident[0:nk, 0:nk],
                                                  rhs=m_ap, start=False, stop=True), [b_const], [slot_b[sp]])
                if blockbias is not None:
                    nb_ = len(blk)
                    pe(lambda e: e.matmul(spt[:, 0:nb_ * 128].rearrange("p (a b) -> p a b", b=128), lhsT=ident[:, :],
                                          rhs=blockbias(bi * 4, nb_, h), start=False, stop=True),
                       [b_const], [slot_b[sp]])
                nfull = sum(1 for (_, nk) in blk if nk == 128)
                kt0 = blk[0][0]
                if nfull > 0:
                    act(lambda e: e.activation(out=PTs[sp][:, 0:nfull * 128].rearrange("p (a b) -> p a b", b=128)[:, :, 0:nq],
                                               in_=spt[:, 0:nfull * 128].rearrange("p (a b) -> p a b", b=128)[:, :, 0:nq],
                                               func=AF.Exp), [slot_b[sp]], [PTb[sp]])
                    if use_mt:
                        dve(lambda e: e.tensor_tensor(
                            out=PTs[sp][:, 0:nfull * 128].rearrange("p (a b) -> p a b", b=128)[:, :, 0:nq],
                            in0=PTs[sp][:, 0:nfull * 128].rearrange("p (a b) -> p a b", b=128)[:, :, 0:nq],
                            in1=MT[:, kt0:kt0 + nfull, 0:nq], op=ALU.mult), [PTb[sp], b_MT], [PTb[sp]])
                for j, (kt, nk) in enumerate(blk):
                    if nk < 128:
                        act(lambda e: e.activation(out=PTs[sp][0:nk, j * 128:j * 128 + nq],
                                                   in_=spt[0:nk, j * 128:j * 128 + nq], func=AF.Exp),
                            [slot_b[sp]], [PTb[sp]])
                        if use_mt:
                            dve(lambda e: e.tensor_tensor(out=PTs[sp][0:nk, j * 128:j * 128 + nq],
                                                          in0=PTs[sp][0:nk, j * 128:j * 128 + nq],
                                                          in1=MT[0:nk, kt, 0:nq], op=ALU.mult),
                                [PTb[sp], b_MT], [PTb[sp]])

            def pv(ui):
                bi, h = units[ui]
                blk = tiles[bi * 4:(bi + 1) * 4]
                sp = ui % NSL
                for j, (kt, nk) in enumerate(blk):
                    gi = bi * 4 + j
                    pe(lambda e: e.matmul(ops[0:nq, h * 65:(h + 1) * 65], lhsT=PTs[sp][0:nk, j * 128:j * 128 + nq],
                                          rhs=V[0:nk, kt, h * 65:(h + 1) * 65], start=(ui == 0 and j == 0),
                                          stop=(ui == len(units) - 1 and gi == nt - 1)),
                       [PTb[sp], b_v[kt]], [b_ops])
                tick()

            NSL = 3 if nq == 128 else 2
            LAG = NSL - 1
            slot_ps = [sps[:, 0:512], sps[:, 512:1024], ips[:, 0:512]]
            slot_b = [b_sps[0], b_sps[1], b_ips]
            PTs = [PT[0], PT[1], biasS[:].rearrange("p a b c -> p (a b c)")[:, 0:640]]
            PTb = [b_PT[0], b_PT[1], b_biasS]
            for ui in range(len(units)):
                scores(ui)
                tick()
                if ui == 2:
                    run_later()
                if ui >= LAG:
                    pv(ui - LAG)
            for ui in range(max(0, len(units) - LAG), len(units)):
                pv(ui)
            ov = ops[0:nq, 0:nheads * 65].rearrange("p (h d) -> p h d", d=65)
            dve(lambda e: e.reciprocal(out=rden[0:nq, 0:nheads], in_=ov[:, :, 64]), [b_ops], [b_rden])
            dve(lambda e: e.tensor_tensor(out=otmp[0:nq, 0:nheads * 64].rearrange("p (h d) -> p h d", d=64),
                                          in0=ov[:, :, 0:64],
                                          in1=rden[0:nq, 0:nheads].unsqueeze(2).to_broadcast([nq, nheads, 64]),
                                          op=ALU.mult), [b_ops, b_rden], [b_otmp])
            dve(lambda e: e.tensor_tensor(out=oall[par][0:nq, ocol0:ocol0 + nheads * 64], in0=otmp[0:nq, 0:nheads * 64],
                                          in1=gates[par][0:nq, ocol0:ocol0 + nheads * 64], op=ALU.mult),
                [b_otmp, b_gates[par]], [b_oall[par]])

        class Tile:
            pass

        def S1(t):
            nq, par, outs, kt_a, kt_b = t.nq, t.par, t.outs, t.kt_a, t.kt_b
            rmsnorm_T(nq, t.x_ap, split=True)
            dmo(lambda e: e.dma_start(out=ropeT[0:nq, :], in_=c_ropeT[t.pos0:t.pos0 + nq, :]), [], [b_rope])
            dmo(lambda e: e.dma_start(out=ropeI[0:nq, :], in_=c_ropeI[t.pos0:t.pos0 + nq, :]), [], [b_rope])
            yield
            yield
            yield
            rmsnorm_T2(nq)
            kcnt = [0]

            def stage_out(z, bz, w, dram_ap, src_is_sb=None):
                s = kcnt[0] % 2
                kcnt[0] += 1
                if src_is_sb is None:
                    act(lambda e: acp(e, out=kst[s][0:nq, 0:w], in_=z[0:nq, 0:w]), [bz], [b_kst[s]])
                dmo(lambda e: e.dma_start(out=dram_ap, in_=kst[s][0:nq, 0:w]), [b_kst[s]], [])
                return s

            def P(g):
                return project(nq, wbfg[g], b_wbf, SEGS[g][1] - SEGS[g][0], g=g)

            z, bz = P(0)

            def post0(z=z, bz=bz):
                act(lambda e: e.mul(out=qtok[0:nq, :], in_=z[0:nq, 0:384], mul=0.125), [bz], [b_qtok])
                transpose_to(nq, qtok, b_qtok, 384, lambda c: qaT[par][:, c, 0:nq], b_qaT[par], defer=True)
            deferred.append(post0)
            yield
            z, bz = P(1)

            def post1(z=z, bz=bz):
                act(lambda e: acp(e, out=ktok[0:nq, :], in_=z[0:nq, 0:384]), [bz], [b_ktok])
                if outs.get("ak") is not None:
                    stage_out(z, bz, 384, outs["ak"])
                transpose_to(nq, ktok, b_ktok, 384, lambda c: kaT[:, c, kt_a * 128:kt_a * 128 + nq], b_kaT[kt_a], defer=True)
            deferred.append(post1)
            yield
            z, bz = P(2)

            def post2(z=z, bz=bz):
                dve(lambda e: e.tensor_copy(out=va[0:nq, kt_a, :].rearrange("p (h d) -> p h d", d=65)[:, :, 0:64],
                                            in_=z[0:nq, 0:384].rearrange("p (h d) -> p h d", d=64)), [bz], [b_va[kt_a]])
                if outs.get("av") is not None:
                    stage_out(z, bz, 384, outs["av"])
            deferred.append(post2)
            yield
            z, bz = P(3)

            def post3(z=z, bz=bz):
                act(lambda e: e.activation(out=gates[par][0:nq, 0:384], in_=z[0:nq, 0:384], func=AF.Silu), [bz], [b_gates[par]])
            deferred.append(post3)
            yield
            z, bz = P(4)

            def post4(z=z, bz=bz):
                rope(nq, z, bz, 6, 32, ropeT, rt1[0:nq, 0:384], b_rt1)
                act(lambda e: e.mul(out=qtok[0:nq, :], in_=rt1[0:nq, 0:384], mul=0.125), [b_rt1], [b_qtok])
                transpose_to(nq, qtok, b_qtok, 384, lambda c: qbT[par][:, c, 0:nq], b_qbT[par], defer=True)
            deferred.append(post4)
            yield
            z, bz = P(5)

            def post5(z=z, bz=bz):
                s_ = kcnt[0] % 2
                rope(nq, z, bz, 6, 32, ropeT, kst[s_][0:nq, 0:384], b_kst[s_])
                act(lambda e: acp(e, out=ktok[0:nq, :], in_=kst[s_][0:nq, 0:384]), [b_kst[s_]], [b_ktok])
                stage_out(None, None, 384, outs["bk"], src_is_sb=True)
                transpose_to(nq, ktok, b_ktok, 384, lambda c: kbT[:, c, kt_b * 128:kt_b * 128 + nq], b_kbT[kt_b], defer=True)
            deferred.append(post5)
            yield
            z, bz = P(6)

            def post6(z=z, bz=bz):
                dve(lambda e: e.tensor_copy(out=vb[0:nq, kt_b, :].rearrange("p (h d) -> p h d", d=65)[:, :, 0:64],
                                            in_=z[0:nq, 0:384].rearrange("p (h d) -> p h d", d=64)), [bz], [b_vb[kt_b]])
                stage_out(z, bz, 384, outs["bv"])
            deferred.append(post6)
            yield
            z, bz = P(7)

            def post7(z=z, bz=bz):
                act(lambda e: e.activation(out=gates[par][0:nq, 384:768], in_=z[0:nq, 0:384], func=AF.Silu), [bz], [b_gates[par]])
            deferred.append(post7)
            yield
            z, bz = P(8)

            def post8(z=z, bz=bz):
                act(lambda e: e.mul(out=qmtok[0:nq, 0:256], in_=z[0:nq, 0:256], mul=0.125), [bz], [b_qmtok])
                act(lambda e: e.activation(out=gates[par][0:nq, 768:1024], in_=z[0:nq, 256:512], func=AF.Silu), [bz], [b_gates[par]])
                transpose_to(nq, qmtok, b_qmtok, 256, lambda c: qmT[par][:, c, 0:nq], b_qmT[par], defer=True)
            deferred.append(post8)
            yield
            z, bz = project(nq, wbfg[9], b_wbf, SEGS[9][1] - SEGS[9][0], g=9)
            rope(nq, z, bz, 8, 16, ropeI, rt1[0:nq, 0:256], b_rt1)
            act(lambda e: acp(e, out=wis[0:nq, :], in_=z[0:nq, 288:296]), [bz], [b_wis])
            zk = z[0:nq, 256:288]
            dve(lambda e: e.tensor_tensor(out=rt1[0:nq, 320:352].rearrange("p (t d) -> p t d", t=2),
                                          in0=zk.rearrange("p (t d) -> p t d", t=2),
                                          in1=ropeI[0:nq, 0:16].unsqueeze(1).to_broadcast([nq, 2, 16]), op=ALU.mult),
                [bz, b_rope], [b_rt1])
            dve(lambda e: e.tensor_tensor(out=rt2[0:nq, 320:336], in0=zk[:, 16:32], in1=ropeI[0:nq, 32:48], op=ALU.mult),
                [bz, b_rope], [b_rt2])
            dve(lambda e: e.tensor_tensor(out=rt2[0:nq, 336:352], in0=zk[:, 0:16], in1=ropeI[0:nq, 16:32], op=ALU.mult),
                [bz, b_rope], [b_rt2])
            dve(lambda e: e.tensor_tensor(out=kis[0:nq, :], in0=rt1[0:nq, 320:352], in1=rt2[0:nq, 320:352], op=ALU.add),
                [b_rt1, b_rt2], [b_kis])
            dmo(lambda e: e.dma_start(out=outs["bi"], in_=kis[0:nq, :]), [b_kis], [])
            act(lambda e: acp(e, out=kib[0:nq, :], in_=kis[0:nq, :]), [b_kis], [b_kib])
            pe(lambda e: e.transpose(out=tps[0:32, 0:nq], in_=kib[0:nq, 0:32], identity=ident[0:nq, 0:nq]),
               [b_kib, b_const], [b_tps])
            act(lambda e: acp(e, out=kiT[0:32, kt_b * 128:kt_b * 128 + nq], in_=tps[0:32, 0:nq]), [b_tps], [b_kiT[kt_b]])
            yield
            act(lambda e: e.activation(out=sgn[0:nq, :], in_=wis[0:nq, :], func=AF.Sign), [b_wis], [b_sgn])
            act(lambda e: e.activation(out=absw[0:nq, :], in_=wis[0:nq, :], func=AF.Abs, scale=0.0625), [b_wis], [b_absw])
            dve(lambda e: e.tensor_tensor(out=qib[0:nq, :].rearrange("p (h d) -> p h d", d=32),
                                          in0=rt1[0:nq, 0:256].rearrange("p (h d) -> p h d", d=32),
                                          in1=absw[0:nq, :].unsqueeze(2).to_broadcast([nq, 8, 32]), op=ALU.mult),
                [b_rt1, b_absw], [b_qib])
            for h in range(8):
                pe(lambda e: e.transpose(out=tps[0:32, h * 128:h * 128 + nq], in_=qib[0:nq, h * 32:(h + 1) * 32],
                                         identity=ident[0:nq, 0:nq]), [b_qib, b_const], [b_tps])
            act(lambda e: acp(e, out=qid[0:32, 0:nq * 8].rearrange("p (t h) -> p h t", h=8),
                              in_=tps[0:32, :].rearrange("p (h t) -> p h t", t=128)[:, :, 0:nq]), [b_tps], [b_qid])
            pool(lambda e: e.tensor_tensor(out=Et[0:nq, :].rearrange("p (t h) -> p t h", h=8),
                                           in0=dmask[0:nq, :].rearrange("p (t h) -> p t h", h=8),
                                           in1=sgn[0:nq, :].unsqueeze(1).to_broadcast([nq, 16, 8]), op=ALU.mult),
                 [b_const, b_sgn], [b_Et])
            pe(lambda e: e.transpose(out=tps2[:, 0:nq], in_=Et[0:nq, :], identity=ident[0:nq, 0:nq]),
               [b_Et, b_const], [b_tps2])
            ng = nq // 16
            for g in range(ng):
                act(lambda e: acp(e, out=Z[:, g, 16 * g:16 * g + 16], in_=tps2[:, 16 * g:16 * g + 16]), [b_tps2], [b_Z])
            flush_deferred()
            flush_deferred()

        def IDX(t):
            nq = t.nq
            ng = nq // 16
            nkeys = t.nkeys
            units = []
            k0 = 0
            while k0 < nkeys:
                k1 = min(k0 + 512, nkeys)
                for g in range(ng):
                    units.append((k0, k1, g))
                k0 = k1

            if nq == 128:
                dsl = [sps[:, 0:512], sps[:, 512:1024], zps[0], zps[1]]
                dsb = [b_sps[0], b_sps[1], b_zps[0], b_zps[1]]
                rsl = [PT[0], PT[1], biasS[:].rearrange("p a b c -> p (a b c)")[:, 0:512],
                       oT[:].rearrange("p a b -> p (a b)")[:, 0:512]]
                rsb = [b_PT[0], b_PT[1], b_biasS, b_oT]
            else:
                dsl = [sps[:, 0:512], sps[:, 512:1024]]
                dsb = [b_sps[0], b_sps[1]]
                rsl = [PT[0], PT[1]]
                rsb = [b_PT[0], b_PT[1]]
            NS = len(dsl)

            def stage1(ui):
                k0, k1, g = units[ui]
                ncol = k1 - k0
                kts = list(range(k0 // 128, (k1 + 127) // 128))
                sp = ui % NS
                spt = dsl[sp]
                pe(lambda e: e.matmul(spt[:, 0:ncol], lhsT=qid[0:32, g * 128:(g + 1) * 128], rhs=kiT[0:32, k0:k1],
                                      start=True, stop=True), [b_qid] + [b_kiT[x] for x in kts], [dsb[sp]])
                if ui % 2 == 0:
                    act(lambda e: e.activation(out=rsl[sp][:, 0:ncol], in_=spt[:, 0:ncol], func=AF.Relu),
                        [dsb[sp]], [rsb[sp]])
                else:
                    dve(lambda e: e.tensor_scalar(out=rsl[sp][:, 0:ncol], in0=spt[:, 0:ncol], scalar1=0.0, scalar2=None,
                                                  op0=ALU.max), [dsb[sp]], [rsb[sp]])

            def stage2(ui):
                k0, k1, g = units[ui]
                ncol = k1 - k0
                sp = ui % NS
                pe(lambda e: e.matmul(ips[0:nq, 0:ncol], lhsT=Z[:, g, 0:nq], rhs=rsl[sp][:, 0:ncol],
                                      start=(g == 0), stop=(g == ng - 1)), [b_Z, rsb[sp]], [b_ips])
                if g == ng - 1:
                    act(lambda e: acp(e, out=Isb[0:nq, k0:k1], in_=ips[0:nq, 0:ncol]), [b_ips], [b_I])

            LG = NS - 1
            for ui in range(len(units)):
                stage1(ui)
                if ui >= LG:
                    stage2(ui - LG)
            for ui in range(max(0, len(units) - LG), len(units)):
                stage2(ui)
            if t.corner is not None:
                dve(lambda e: e.memset(Isb[0:64, t.corner:t.corner + 64], NBIG), [], [b_I])

        def THR(t):
            nq, nkeys, n_safe = t.nq, t.nkeys, t.n_safe
            THRc = sm[0:nq, 26:27]
            if not t.bisect:
                dve(lambda e: e.memset(THRc, -1.0e29), [], [b_smt])
                yield
            else:
                dve(lambda e: e.tensor_reduce(out=sm[0:nq, 20:21], in_=Isb[0:nq, 0:nkeys], axis=AX.X, op=ALU.max), [b_I], [b_smt])
                yield
                dve(lambda e: e.tensor_reduce(out=sm[0:nq, 21:22], in_=Isb[0:nq, 0:n_safe], axis=AX.X, op=ALU.min), [b_I], [b_smt])
                yield
                dve(lambda e: e.tensor_tensor(out=sm[0:nq, 22:23], in0=sm[0:nq, 20:21], in1=sm[0:nq, 21:22], op=ALU.subtract), [b_smt], [b_smt])
                yield
                dve(lambda e: e.tensor_scalar(out=sm[0:nq, 23:24], in0=sm[0:nq, 22:23], scalar1=1.0001, scalar2=1e-20,
                                              op0=ALU.mult, op1=ALU.add), [b_smt], [b_smt])
                yield
                dve(lambda e: e.tensor_scalar(out=halfs[0:nq, :], in0=pow2[0:nq, :], scalar1=sm[0:nq, 23:24], scalar2=None,
                                              op0=ALU.mult), [b_smt, b_const], [b_halfs])
                yield
                dve(lambda e: e.memset(cntb[0:nq, :], 0.0), [], [b_cnt])
                yield
                MID = sm[0:nq, 24:25]
                dve(lambda e: e.tensor_tensor(out=MID, in0=sm[0:nq, 21:22], in1=halfs[0:nq, 1:2], op=ALU.add), [b_smt, b_halfs], [b_smt])
                yield
                for k in range(1, KIT + 1):
                    dve(lambda e: e.tensor_scalar(out=Mb[0:nq, 0:nkeys], in0=Isb[0:nq, 0:nkeys], scalar1=MID, scalar2=0.0,
                                                  op0=ALU.is_ge, op1=ALU.add, accum_out=cntb[0:nq, k:k + 1]),
                        [b_I, b_smt], [b_Mb, b_cnt])
                    yield
                    if k < KIT:
                        dve(lambda e: e.tensor_scalar(out=sm[0:nq, 25:26], in0=cntb[0:nq, k:k + 1], scalar1=255.5,
                                                      scalar2=halfs[0:nq, k:k + 1], op0=ALU.is_ge, op1=ALU.mult),
                            [b_cnt, b_halfs], [b_smt])
                        yield
                        dve(lambda e: e.scalar_tensor_tensor(out=MID, in0=sm[0:nq, 25:26], scalar=halfs[0:nq, k + 1:k + 2],
                                                             in1=MID, op0=ALU.subtract, op1=ALU.add),
                            [b_smt, b_halfs], [b_smt])
                        yield
                    else:
                        dve(lambda e: e.tensor_scalar(out=sm[0:nq, 25:26], in0=cntb[0:nq, k:k + 1], scalar1=255.5,
                                                      scalar2=halfs[0:nq, k:k + 1], op0=ALU.is_lt, op1=ALU.mult),
                            [b_cnt, b_halfs], [b_smt])
                        yield
                        dve(lambda e: e.tensor_tensor(out=THRc, in0=MID, in1=sm[0:nq, 25:26], op=ALU.subtract), [b_smt], [b_smt])
                        yield
            dve(lambda e: e.tensor_scalar(out=Mb[0:nq, 0:nkeys], in0=Isb[0:nq, 0:nkeys], scalar1=THRc, scalar2=0.0,
                                          op0=ALU.is_ge, op1=ALU.add), [b_I, b_smt], [b_Mb])
            yield

        def AM(t):
            attend(t.nq, t.par, 6, qaT[t.par], b_qaT[t.par], kaT, b_kaT, va, b_va, t.a_tiles, t.a_bias, t.a_mask, False, 0,
                   blockbias=t.a_blockbias)
            attend(t.nq, t.par, 4, qmT[t.par], b_qmT[t.par], mkT, [b_mk, b_mk], mv, [b_mv, b_mv], [(0, 128), (1, 128)], None, None, False, 768)

        def MTB(t):
            nq = t.nq
            b_tiles = t.b_tiles
            for j0 in range(0, len(b_tiles), 8):
                js = b_tiles[j0:j0 + 8]
                for jj, (kt, nk) in enumerate(js):
                    pe(lambda e: e.transpose(out=tps2[0:nk, jj * 128:jj * 128 + nq], in_=Mb[0:nq, kt * 128:kt * 128 + nk],
                                             identity=ident[0:nq, 0:nq]), [b_Mb, b_const], [b_tps2])
                nfull = sum(1 for (_, nk) in js if nk == 128)
                if nfull > 0:
                    dve(lambda e: e.tensor_copy(out=MT[:, js[0][0]:js[0][0] + nfull, 0:nq],
                                                in_=tps2[:, 0:nfull * 128].rearrange("p (a b) -> p a b", b=128)[:, :, 0:nq]),
                        [b_tps2], [b_MT])
                for jj, (kt, nk) in enumerate(js):
                    if nk < 128:
                        dve(lambda e: e.tensor_copy(out=MT[0:nk, kt, 0:nq], in_=tps2[0:nk, jj * 128:jj * 128 + nq]), [b_tps2], [b_MT])

        def BOUT2(t):
            nq, par = t.nq, t.par
            b_tiles = t.b_tiles
            dma(lambda e: e.dma_start(out=xres[0:nq, :], in_=t.x_ap), [], [b_xres])
            attend(nq, par, 6, qbT[par], b_qbT[par], kbT, b_kbT, vb, b_vb, b_tiles, None, None, True, 384)
            later.append(lambda: OUTP(t))

        def OUTP(t):
            nq, par = t.nq, t.par
            flush_deferred()
            for c in range(8):
                pe(lambda e: e.transpose(out=tps[:, c * 128:c * 128 + nq], in_=oall[par][0:nq, c * 128:(c + 1) * 128],
                                         identity=ident[0:nq, 0:nq]), [b_oall[par], b_const], [b_tps])
            act(lambda e: acp(e, out=oT[:, :, 0:nq], in_=tps[:].rearrange("p (a b) -> p a b", b=128)[:, :, 0:nq]),
                [b_tps], [b_oT])
            for cg in range(2):
                for k in range(8):
                    pe(lambda e: e.matmul(zps[cg][0:nq, :], lhsT=oT[:, k, 0:nq], rhs=wo[:, k, cg * 512:(cg + 1) * 512],
                                          start=(k == 0), stop=(k == 7)), [b_oT, b_wo], [b_zps[cg]])
                dve(lambda e: e.tensor_tensor(out=xres[0:nq, cg * 512:(cg + 1) * 512], in0=zps[cg][0:nq, :],
                                              in1=xres[0:nq, cg * 512:(cg + 1) * 512], op=ALU.add), [b_zps[cg], b_xres], [b_xres])
            act(lambda e: e.activation(out=oall[par][0:nq, :], in_=xres[0:nq, :], func=AF.Square, accum_out=sm[0:nq, 12:13]),
                [b_xres], [b_oall[par], b_smo])
            dve(lambda e: e.tensor_scalar(out=sm[0:nq, 13:14], in0=sm[0:nq, 12:13], scalar1=1.0 / D, scalar2=1e-6,
                                          op0=ALU.mult, op1=ALU.add), [b_smo], [b_smo])
            act(lambda e: e.activation(out=sm[0:nq, 14:15], in_=sm[0:nq, 13:14], func=AF.Sqrt), [b_smo], [b_smo])
            dve(lambda e: e.reciprocal(out=sm[0:nq, 15:16], in_=sm[0:nq, 14:15]), [b_smo], [b_smo])
            dve(lambda e: e.scalar_tensor_tensor(out=xres[0:nq, :], in0=xres[0:nq, :], scalar=sm[0:nq, 15:16], in1=gfb[0:nq, :],
                                                 op0=ALU.mult, op1=ALU.mult), [b_xres, b_smo, b_const], [b_xres])
            dmo(lambda e: e.dma_start(out=t.outs["y"], in_=xres[0:nq, :]), [b_xres], [])

        def load_cache_kv(src_k, src_v, ntile, kT, b_kT, V, b_V, ncols, nh):
            for j in range(ntile):
                s = j % 2
                dma(lambda e: e.dma_start(out=kst[s][:, 0:ncols], in_=src_k[j * 128:(j + 1) * 128, :]), [], [b_kst[s]])
                act(lambda e: acp(e, out=ktok[:, 0:ncols], in_=kst[s][:, 0:ncols]), [b_kst[s]], [b_ktok])
                transpose_to(128, ktok, b_ktok, ncols, lambda c: kT[:, c, j * 128:(j + 1) * 128], b_kT[j])
                dma(lambda e: e.dma_start(out=kst[s][:, 0:ncols], in_=src_v[j * 128:(j + 1) * 128, :]), [], [b_kst[s]])
                dve(lambda e: e.tensor_copy(out=V[:, j, 0:nh * 65].rearrange("p (h d) -> p h d", d=65)[:, :, 0:64],
                                            in_=kst[s][:, 0:ncols].rearrange("p (h d) -> p h d", d=64)), [b_kst[s]], [b_V[j]])

        def cache_b_tile(j):
            s_ = j % 2
            flush_deferred()
            dma(lambda e: e.dma_start(out=kst[s_][:, 0:384], in_=cb_k[j * 128:(j + 1) * 128, :]), [], [b_kst[s_]])
            act(lambda e: acp(e, out=ktok[:, 0:384], in_=kst[s_][:, 0:384]), [b_kst[s_]], [b_ktok])
            transpose_to(128, ktok, b_ktok, 384, lambda c: kbT[:, c, j * 128:(j + 1) * 128], b_kbT[j])
            dma(lambda e: e.dma_start(out=kst[s_][:, 0:384], in_=cb_v[j * 128:(j + 1) * 128, :]), [], [b_kst[s_]])
            dve(lambda e: e.tensor_copy(out=vb[:, j, 0:390].rearrange("p (h d) -> p h d", d=65)[:, :, 0:64],
                                        in_=kst[s_][:, 0:384].rearrange("p (h d) -> p h d", d=64)), [b_kst[s_]], [b_vb[j]])
            dma(lambda e: e.dma_start(out=kis[:, :], in_=cb_ki[j * 128:(j + 1) * 128, :]), [], [b_kis])
            act(lambda e: acp(e, out=kib[:, :], in_=kis[:, :]), [b_kis], [b_kib])
            pe(lambda e: e.transpose(out=tps[0:32, 0:128], in_=kib[:, 0:32], identity=ident[:, :]), [b_kib, b_const], [b_tps])
            act(lambda e: acp(e, out=kiT[0:32, j * 128:(j + 1) * 128], in_=tps[0:32, 0:128]), [b_tps], [b_kiT[j]])

        def CACHE_HI():
            for j in range(16, 32):
                cache_b_tile(j)
                yield
                yield
                yield

        def run_pipeline(tiles):
            if not tiles:
                return
            for _ in S1(tiles[0]):
                pass
            IDX(tiles[0])
            prev = None
            for n, t in enumerate(tiles):
                bg.append(THR(t))
                if prev is not None:
                    BOUT2(prev)
                if n + 1 < len(tiles):
                    bg.append(S1(tiles[n + 1]))
                AM(t)
                drain_bg()
                run_later()
                if n + 1 < len(tiles):
                    IDX(tiles[n + 1])
                MTB(t)
                prev = t
            BOUT2(prev)
            run_later()

        if DBG["sample"]:
            t = Tile()
            t.nq = T; t.par = 0; t.x_ap = xs[:, :]; t.pos0 = PAST; t.kt_a = 4; t.kt_b = 32
            t.a_tiles = [(0, 128), (1, 128), (2, 128), (3, 128), (4, T)]
            t.a_bias = lambda ti, h, nk, nq: biasS[0:nk, ti, h, 0:nq]
            t.a_mask = None
            t.a_blockbias = None
            t.b_tiles = [(j, 128) for j in range(32)] + [(32, T)]
            t.nkeys = PAST + T; t.corner = None; t.n_safe = PAST + T; t.bisect = True
            t.outs = {"y": y_s[:, :], "ak": aks[:, :], "av": avs[:, :], "bk": bks[:, :], "bv": bvs[:, :], "bi": bis[:, :]}
            run_pipeline([t])
        for b in range(DBG["nb"]):
            for mt in range(2 if DBG["mem"] else 0):
                rmsnorm_T(128, memp[b, mt * 128:(mt + 1) * 128, :])
                z, bz = project(128, wmbf, b_wmbf, 512)
                act(lambda e: acp(e, out=kst[0][:, 0:256], in_=z[:, 0:256]), [bz], [b_kst[0]])
                dmo(lambda e: e.dma_start(out=mkp[b, mt * 128:(mt + 1) * 128, :], in_=kst[0][:, 0:256]), [b_kst[0]], [])
                act(lambda e: acp(e, out=kst[1][:, 0:256], in_=z[:, 256:512]), [bz], [b_kst[1]])
                dmo(lambda e: e.dma_start(out=mvp[b, mt * 128:(mt + 1) * 128, :], in_=kst[1][:, 0:256]), [b_kst[1]], [])
                act(lambda e: acp(e, out=ktok[:, 0:256], in_=z[:, 0:256]), [bz], [b_ktok])
                transpose_to(128, ktok, b_ktok, 256, lambda c: mkT[:, c, mt * 128:(mt + 1) * 128], b_mk)
                dve(lambda e: e.tensor_copy(out=mv[:, mt, :].rearrange("p (h d) -> p h d", d=65)[:, :, 0:64],
                                            in_=z[:, 256:512].rearrange("p (h d) -> p h d", d=64)), [bz], [b_mv])
            tl = []
            for i in range(DBG["nt"]):
                t = Tile()
                t.nq = 128; t.par = i % 2; t.x_ap = xp[b, i * 128:(i + 1) * 128, :]; t.pos0 = i * 128
                t.kt_a = i; t.kt_b = i
                t.a_tiles = [(kt, 128) for kt in range(max(0, i - 4), i + 1)]
                kt0 = t.a_tiles[0][0]
                nta = len(t.a_tiles)
                t.a_bias = None
                t.a_mask = None
                t.a_blockbias = (lambda g0, nb_, h, nta=nta: biasP[:, h, 5 - nta + g0:5 - nta + g0 + nb_, :])
                t.b_tiles = [(j, 128) for j in range(i + 1)]
                t.nkeys = (i + 1) * 128
                t.corner = i * 128 + 64; t.n_safe = i * 128 + 64; t.bisect = i >= 2
                t.outs = {"y": y_p[b, i * 128:(i + 1) * 128, :], "bk": bkp[b, i * 128:(i + 1) * 128, :],
                          "bv": bvp[b, i * 128:(i + 1) * 128, :], "bi": bip[b, i * 128:(i + 1) * 128, :]}
                if i >= 12:
                    t.outs["ak"] = akp[b, (i - 12) * 128:(i - 11) * 128, :]
                    t.outs["av"] = avp[b, (i - 12) * 128:(i - 11) * 128, :]
                tl.append(t)
            run_pipeline(tl)

        P.finish()
        print("instructions:", P.nins, "sbuf remaining:", nc.sbuf_bytes_remaining)
    return nc


def _consts(rel_bias):
    ident = np.eye(128, dtype=np.float32).astype(ml_dtypes.bfloat16)
    dm = np.zeros((128, 16, 8), np.float32)
    for p in range(128):
        dm[p, p % 16, :] = 1.0
    dmask = dm.reshape(128, 128).astype(ml_dtypes.bfloat16)
    pow2 = np.tile((2.0 ** -np.arange(KIT + 1, dtype=np.float64)).astype(np.float32)[None, :], (128, 1))
    mA = np.zeros((128, 5, 128), np.float32)
    mA[64:128, 4, 0:64] = NEG
    mA[0:64, 0, 64:128] = NEG
    maskA = mA.reshape(128, 640).astype(ml_dtypes.bfloat16)

    def rope_tab(d, npos):
        half = d // 2
        inv = (np.float32(10000.0) ** (-np.arange(half, dtype=np.float32) * np.float32(2.0) / np.float32(d))).astype(np.float32)
        ang = np.arange(npos, dtype=np.float32)[:, None] * inv[None, :]
        c = np.cos(ang).astype(np.float32)
        s = np.sin(ang).astype(np.float32)
        return np.ascontiguousarray(np.concatenate([c, s, -s], axis=1).astype(np.float32))

    ropeT = rope_tab(64, PAST + 128)
    ropeI = rope_tab(32, PAST + 128)
    tab = np.asarray(rel_bias, np.float32)
    a = np.arange(128)[:, None, None]
    m = np.arange(5)[None, :, None]
    bq = np.arange(128)[None, None, :]
    idx = np.clip(128 * (4 - m) + bq - a, -256, 256) + 256
    biasP = np.ascontiguousarray(np.transpose(tab[:, idx], (1, 0, 2, 3))).reshape(128, 6 * 5 * 128)
    key = np.arange(640).reshape(5, 128)
    tq = np.arange(32)
    idx2 = np.clip(512 + tq[None, None, :] - key[:, :, None], -256, 256) + 256
    bs = tab[:, idx2]
    biasS = np.ascontiguousarray(np.transpose(bs, (2, 1, 0, 3))).reshape(128, 5 * 6 * 32)
    return dict(c_ident=ident, c_dmask=dmask, c_pow2=pow2, c_maskA=maskA, c_ropeT=ropeT, c_ropeI=ropeI,
                c_biasP=biasP.astype(np.float32), c_biasS=biasS.astype(np.float32))


_NC = None


def kernel(x_prompt, x_sample, mem_prompt, cache_a_k, cache_a_v, cache_b_k, cache_b_v, cache_b_kidx,
           cache_mem_k, cache_mem_v, norm_mix_g, w_in, rel_bias_a, norm_mem_g, w_mem_kv, w_out, norm_final_g):
    global _NC
    f = lambda a: np.ascontiguousarray(np.asarray(a, dtype=np.float32))
    x_prompt, x_sample, mem_prompt = f(x_prompt), f(x_sample), f(mem_prompt)
    consts = _consts(np.asarray(rel_bias_a)[0])
    if _NC is None:
        _NC = build_nc()
    nc = _NC
    in_maps = []
    for c in range(8):
        m = dict(consts)
        m.update(
            xp=x_prompt[2 * c:2 * c + 2], xs=x_sample[c], memp=mem_prompt[2 * c:2 * c + 2],
            ca_k=f(cache_a_k)[0, c].reshape(512, 384), ca_v=f(cache_a_v)[0, c].reshape(512, 384),
            cb_k=f(cache_b_k)[0, c].reshape(PAST, 384), cb_v=f(cache_b_v)[0, c].reshape(PAST, 384),
            cb_ki=f(cache_b_kidx)[0, c], cm_k=f(cache_mem_k)[0, c].reshape(256, 256),
            cm_v=f(cache_mem_v)[0, c].reshape(256, 256),
            g_mix=f(norm_mix_g)[0], w_in=f(w_in)[0], g_mem=f(norm_mem_g)[0], w_mem=f(w_mem_kv)[0],
            w_out=f(w_out)[0], g_fin=f(norm_final_g),
        )
        in_maps.append({k: np.ascontiguousarray(v) for k, v in m.items()})
    res = run_bass_kernel_spmd(nc, in_maps, core_ids=list(range(8)))
    R = res.results
    cat = lambda k: np.concatenate([np.asarray(r[k]) for r in R], axis=0)
    stk = lambda k: np.stack([np.asarray(r[k]) for r in R], axis=0)
    y_prompt = cat("y_p").reshape(16, S, D)
    y_sample = stk("y_s").reshape(8, T, D)
    out = (
        y_prompt, y_sample,
        cat("akp").reshape(1, 16, 512, 6, 64), cat("avp").reshape(1, 16, 512, 6, 64),
        cat("bkp").reshape(1, 16, S, 6, 64), cat("bvp").reshape(1, 16, S, 6, 64),
        cat("bip").reshape(1, 16, S, 32),
        cat("mkp").reshape(1, 16, 256, 4, 64), cat("mvp").reshape(1, 16, 256, 4, 64),
        stk("aks").reshape(1, 8, T, 6, 64), stk("avs").reshape(1, 8, T, 6, 64),
        stk("bks").reshape(1, 8, T, 6, 64), stk("bvs").reshape(1, 8, T, 6, 64),
        stk("bis").reshape(1, 8, T, 32),
    )
    return tuple(np.ascontiguousarray(o.astype(np.float32)) for o in out)
```

```python
from contextlib import ExitStack
import numpy as np
import ml_dtypes
import concourse.bass as bass
import concourse.mybir as mybir
from concourse.bass_utils import run_bass_kernel_spmd

F32 = mybir.dt.float32
BF16 = mybir.dt.bfloat16
ALU = mybir.AluOpType
AF = mybir.ActivationFunctionType
AX = mybir.AxisListType

D = 1024
S = 2048
NT = 16
T = 32
PAST = 4096
INC = 3880
NEG = -30000.0
NBIG = -1.0e30
KIT = 16
NKB = 33
SEGS = [(0, 384), (384, 768), (768, 1152), (1152, 1536), (1536, 1920), (1920, 2304),
        (2304, 2688), (2688, 3072), (3072, 3584), (3584, 3880)]


def acp(e, out, in_):
    return e.mul(out=out, in_=in_, mul=1.0)


class Buf:
    def __init__(self, name="", excl=False):
        self.name = name
        self.lw = None
        self.rd = {}
        self.excl = excl


class Prog:
    NDS = 16

    def __init__(self, nc, st):
        self.nc = nc
        self.E = {"pe": nc.tensor, "act": nc.scalar, "dve": nc.vector, "pool": nc.gpsimd, "sp": nc.sync}
        keys = ["pe", "act", "dve", "pool"] + ["d%d" % i for i in range(self.NDS)] + ["q%d" % i for i in range(self.NDS)]
        self.sem = {k: st.enter_context(nc.semaphore("s_" + k)) for k in keys}
        self.cnt = {k: 0 for k in keys}
        self.waited = {e: {k: 0 for k in keys} for e in self.E}
        self.nins = 0
        self.ndma = 0
        self.nsw = 0

    def _wait(self, eng, key, v):
        if self.waited[eng][key] >= v:
            return
        self.E[eng].wait_ge(self.sem[key], v)
        self.waited[eng][key] = v

    def op(self, eng, fn, reads=(), writes=(), dma=False):
        deps = {}
        for b in reads:
            if b.lw is not None:
                deps[b.lw[0]] = max(deps.get(b.lw[0], 0), b.lw[1])
            if b.excl:
                for e, v in b.rd.items():
                    if e != eng:
                        deps[e] = max(deps.get(e, 0), v)
        for b in writes:
            if b.lw is not None:
                deps[b.lw[0]] = max(deps.get(b.lw[0], 0), b.lw[1])
            for e, v in b.rd.items():
                deps[e] = max(deps.get(e, 0), v)
        for p, v in deps.items():
            if p == eng and eng == "pe":
                continue
            self._wait(eng, p, v)
        if dma:
            if eng == "pool":
                key = "q%d" % (self.nsw % self.NDS)
                self.nsw += 1
            else:
                key = "d%d" % (self.ndma % self.NDS)
                self.ndma += 1
            if self.cnt[key] > 0:
                self._wait(eng, key, self.cnt[key])
            inc = 16
        else:
            key = eng
            inc = 1
        self.cnt[key] += inc
        val = self.cnt[key]
        fn(self.E[eng]).then_inc(self.sem[key], inc)
        self.nins += 1
        for b in reads:
            b.rd[key] = max(b.rd.get(key, 0), val)
        for b in writes:
            b.lw = (key, val)
            b.rd = {}

    def dma_barrier(self, eng):
        for k in self.cnt:
            if (k.startswith("d") or k.startswith("q")) and k not in ("dve",) and self.cnt[k] > 0:
                self._wait(eng, k, self.cnt[k])

    def finish(self):
        for k in self.cnt:
            if self.cnt[k] > 0:
                self._wait("sp", k, self.cnt[k])


DBG = {"nb": 2, "nt": NT, "sample": True, "mem": True, "setup": 99}


def build_nc():
    nc = bass.Bass("TRN2", target_bir_lowering=False)
    dt = nc.dram_tensor

    def din(name, shape, dtype=F32):
        return dt(name, list(shape), dtype, kind="ExternalInput").ap()

    def dout(name, shape):
        return dt(name, list(shape), F32, kind="ExternalOutput").ap()

    xp = din("xp", [2, S, D]); xs = din("xs", [T, D]); memp = din("memp", [2, 256, D])
    ca_k = din("ca_k", [512, 384]); ca_v = din("ca_v", [512, 384])
    cb_k = din("cb_k", [PAST, 384]); cb_v = din("cb_v", [PAST, 384]); cb_ki = din("cb_ki", [PAST, 32])
    cm_k = din("cm_k", [256, 256]); cm_v = din("cm_v", [256, 256])
    g_mix = din("g_mix", [D]); w_in = din("w_in", [D, INC]); g_mem = din("g_mem", [D])
    w_mem = din("w_mem", [D, 512]); w_out = din("w_out", [D, D]); g_fin = din("g_fin", [D])
    c_ident = din("c_ident", [128, 128], BF16); c_dmask = din("c_dmask", [128, 128], BF16)
    c_pow2 = din("c_pow2", [128, KIT + 1]); c_maskA = din("c_maskA", [128, 5 * 128], BF16)
    c_ropeT = din("c_ropeT", [PAST + 128, 96]); c_ropeI = din("c_ropeI", [PAST + 128, 48])
    c_biasP = din("c_biasP", [128, 6 * 5 * 128]); c_biasS = din("c_biasS", [128, 5 * 6 * 32])

    y_p = dout("y_p", [2, S, D]); y_s = dout("y_s", [T, D])
    akp = dout("akp", [2, 512, 384]); avp = dout("avp", [2, 512, 384])
    bkp = dout("bkp", [2, S, 384]); bvp = dout("bvp", [2, S, 384]); bip = dout("bip", [2, S, 32])
    mkp = dout("mkp", [2, 256, 256]); mvp = dout("mvp", [2, 256, 256])
    aks = dout("aks", [T, 384]); avs = dout("avs", [T, 384]); bks = dout("bks", [T, 384])
    bvs = dout("bvs", [T, 384]); bis = dout("bis", [T, 32])

    wbfg = [dt("wbf%d" % g, [128, 8, c1 - c0], BF16).ap() for g, (c0, c1) in enumerate(SEGS)]
    wmbf = dt("wmbf", [128, 8, 512], BF16).ap()

    st = ExitStack()
    with st:
        def sb(name, shape, dtype):
            return st.enter_context(nc.sbuf_tensor(name, list(shape), dtype))

        def ps(name, shape, dtype):
            return st.enter_context(nc.psum_tensor(name, list(shape), dtype))

        P = Prog(nc, st)
        kaT = sb("kaT", [128, 3, S], BF16); va = sb("va", [128, NT, 390], BF16)
        kbT = sb("kbT", [128, 3, NKB * 128], BF16); vb = sb("vb", [128, NKB, 390], BF16)
        kiT = sb("kiT", [32, NKB * 128], BF16)
        mkT = sb("mkT", [128, 2, 256], BF16); mv = sb("mv", [128, 2, 260], BF16)
        Isb = sb("Isb", [128, NKB * 128], F32); Mb = sb("Mb", [128, NKB * 128], BF16)
        MT = sb("MT", [128, NKB, 128], BF16)
        wo = sb("wo", [128, 8, D], BF16)
        wch = [sb("wch%d" % i, [128, 8 * 512], BF16) for i in range(2)]
        xt = sb("xt", [128, D], F32); xn = sb("xn", [128, D], BF16); xnT = sb("xnT", [128, 8, 128], BF16)
        biasP = sb("biasP", [128, 6, 5, 128], BF16); biasS = sb("biasS", [128, 5, 6, 32], BF16)
        maskA = sb("maskA", [128, 5, 128], BF16); ident = sb("ident", [128, 128], BF16)
        dmask = sb("dmask", [128, 128], BF16); gfb = sb("gfb", [128, D], F32)
        xres = sb("xres", [128, D], F32)
        kst = [sb("kst%d" % i, [128, 384], F32) for i in range(2)]
        qtok = sb("qtok", [128, 384], BF16)
        qmtok = qtok
        ktok = qtok
        gates = [sb("gates%d" % i, [128, D], BF16) for i in range(2)]
        oall = [sb("oall%d" % i, [128, D], BF16) for i in range(2)]
        rt1 = sb("rt1", [128, 384], F32); rt2 = sb("rt2", [128, 384], F32)
        ropeT = sb("ropeT", [128, 96], F32); ropeI = sb("ropeI", [128, 48], F32)
        qaT = [sb("qaT%d" % i, [128, 3, 128], BF16) for i in range(2)]; qbT = [sb("qbT%d" % i, [128, 3, 128], BF16) for i in range(2)]
        qmT = [sb("qmT%d" % i, [128, 2, 128], BF16) for i in range(2)]
        qib = sb("qib", [128, 256], BF16); qid = sb("qid", [32, 1024], BF16)
        oT = sb("oT", [128, 8, 128], BF16)
        kis = sb("kis", [128, 32], F32); kib = sb("kib", [128, 32], BF16)
        wis = sb("wis", [128, 8], F32); sgn = sb("sgn", [128, 8], BF16); absw = sb("absw", [128, 8], F32)
        Et = sb("Et", [128, 128], BF16); Z = sb("Z", [128, 8, 128], BF16)
        PT = [sb("PT%d" % i, [128, 640], BF16) for i in range(2)]
        otmp = sb("otmp", [128, 384], F32); rden = sb("rden", [128, 8], F32)
        sm = sb("sm", [128, 64], F32)
        halfs = sb("halfs", [128, KIT + 1], F32); pow2 = sb("pow2", [128, KIT + 1], F32)
        cntb = sb("cntb", [128, KIT + 2], F32)
        gcol = sb("gcol", [128, 16], F32)
        zps = [ps("zps%d" % i, [128, 512], F32) for i in range(2)]
        tps = ps("tps", [128, 1024], BF16)
        sps = ps("sps", [128, 1024], F32)
        ips = ps("ips", [128, 512], F32)
        ops = ps("ops", [128, 512], F32)
        tps2 = ps("tps2", [128, 1024], BF16)

        B = lambda n: Buf(n)
        b_kaT = [B("kaT") for _ in range(NT)]; b_va = [B("va") for _ in range(NT)]
        b_kbT = [B("kbT") for _ in range(NKB)]; b_vb = [B("vb") for _ in range(NKB)]
        b_kiT = [B("kiT") for _ in range(NKB)]
        b_mk = B("mk"); b_mv = B("mv"); b_I = B("I"); b_Mb = B("Mb"); b_MT = B("MT"); b_wo = B("wo")
        b_wch = [B("wch0"), B("wch1")]; b_xt = B("xt"); b_xn = B("xn"); b_xnT = B("xnT")
        b_const = B("const"); b_biasS = B("biasS"); b_xres = B("xres"); b_kst = [B("kst0"), B("kst1")]
        b_qtok = B("qtok"); b_qmtok = b_qtok; b_ktok = b_qtok; b_gates = [B("gates0"), B("gates1")]; b_oall = [B("oall0"), B("oall1")]
        b_rt1 = B("rt1"); b_rt2 = B("rt2"); b_rope = B("rope")
        b_qaT = [B("qaT0"), B("qaT1")]; b_qbT = [B("qbT0"), B("qbT1")]; b_qmT = [B("qmT0"), B("qmT1")]; b_oT = B("oT"); b_qib = B("qib"); b_qid = B("qid")
        b_kis = B("kis"); b_kib = B("kib"); b_wis = B("wis"); b_sgn = B("sgn"); b_absw = B("absw")
        b_Et = B("Et"); b_Z = B("Z"); b_PT = [B("PT0"), B("PT1")]; b_otmp = B("otmp"); b_rden = B("rden")
        b_smn = B("smn"); b_smt = B("smt"); b_smo = B("smo"); b_halfs = B("halfs"); b_cnt = B("cnt")
        BX = lambda n: Buf(n, excl=True)
        b_zps = [BX("zps0"), BX("zps1")]; b_tps = BX("tps"); b_sps = [BX("sps0"), BX("sps1")]; b_ips = BX("ips")
        b_ops = BX("ops"); b_tps2 = BX("tps2")
        b_wbf = B("wbf"); b_wmbf = B("wmbf")
        b_out = B("out")

        dma = lambda fn, r, w: P.op("sp", fn, r, w, dma=True)
        dmo = lambda fn, r, w: P.op("pool", fn, r, w, dma=True)
        dma2 = lambda fn, r, w: P.op("act", fn, r, w, dma=True)
        act = lambda fn, r, w: P.op("act", fn, r, w)
        dve = lambda fn, r, w: P.op("dve", fn, r, w)
        pool = lambda fn, r, w: P.op("pool", fn, r, w)
        pe = lambda fn, r, w: P.op("pe", fn, r, w)

        dma(lambda e: e.dma_start(out=ident[:], in_=c_ident[:, :]), [], [b_const])
        dma(lambda e: e.dma_start(out=dmask[:], in_=c_dmask[:, :]), [], [b_const])
        dma(lambda e: e.dma_start(out=pow2[:], in_=c_pow2[:, :]), [], [b_const])
        dma(lambda e: e.dma_start(out=maskA[:].rearrange("p a b -> p (a b)"), in_=c_maskA[:, :]), [], [b_const])
        dma(lambda e: e.dma_start(out=gfb[:], in_=g_fin.partition_broadcast(128)), [], [b_const])
        dma(lambda e: e.dma_start(out=gcol[:, 0:8], in_=g_mix.rearrange("(k p) -> p k", p=128),
                                  allow_slow_non_contiguous=True), [], [b_const])
        dma(lambda e: e.dma_start(out=gcol[:, 8:16], in_=g_mem.rearrange("(k p) -> p k", p=128),
                                  allow_slow_non_contiguous=True), [], [b_const])
        if DBG["setup"] <= 1:
            P.finish()
            return nc
        pool(lambda e: e.memset(va[:].rearrange("p a b -> p (a b)"), 1.0), [], b_va)
        pool(lambda e: e.memset(vb[:].rearrange("p a b -> p (a b)"), 1.0), [], b_vb)
        pool(lambda e: e.memset(mv[:].rearrange("p a b -> p (a b)"), 1.0), [], [b_mv])
        pool(lambda e: e.memset(Z[:].rearrange("p a b -> p (a b)"), 0.0), [], [b_Z])

        NPC = 5
        b_Ih = [B("Ih%d" % i) for i in range(NPC)]; b_Mh = [B("Mh%d" % i) for i in range(NPC)]
        dma(lambda e: e.dma_start(out=Isb[:, 0:3840], in_=c_biasP[:, :]), [], list(b_Ih))
        dve(lambda e: e.tensor_copy(out=biasP[:].rearrange("p a b c -> p (a b c)"), in_=Isb[:, 0:3840]), list(b_Ih), [b_const])
        for h in range(6):
            dve(lambda e: e.tensor_tensor(out=biasP[:, h, :, :], in0=biasP[:, h, :, :], in1=maskA[:, :, :], op=ALU.add),
                [b_const], [b_const])
        dma(lambda e: e.dma_start(out=Isb[:, 0:960], in_=c_biasS[:, :]), [], list(b_Ih))
        dve(lambda e: e.tensor_copy(out=biasS[:].rearrange("p a b c -> p (a b c)"), in_=Isb[:, 0:960]), list(b_Ih), [b_const, b_biasS])
        if DBG["setup"] <= 2:
            P.finish()
            return nc
        def cache_tile(src_k, src_v, j, kT, b_kTj, V, b_Vj, ncols, nh, src_ki=None):
            s_ = j % 2
            dmo(lambda e: e.dma_start(out=kst[s_][:, 0:ncols], in_=src_k[j * 128:(j + 1) * 128, :]), [], [b_kst[s_]])
            act(lambda e: acp(e, out=qtok[:, 0:ncols], in_=kst[s_][:, 0:ncols]), [b_kst[s_]], [b_qtok])
            nch = ncols // 128
            for c in range(nch):
                pe(lambda e: e.transpose(out=tps[:, c * 128:(c + 1) * 128], in_=qtok[:, c * 128:(c + 1) * 128],
                                         identity=ident[:, :]), [b_qtok, b_const], [b_tps])
            act(lambda e: acp(e, out=kT[:, 0:nch, j * 128:(j + 1) * 128],
                              in_=tps[:, 0:nch * 128].rearrange("p (a b) -> p a b", b=128)), [b_tps], [b_kTj])
            dmo(lambda e: e.dma_start(out=kst[s_][:, 0:ncols], in_=src_v[j * 128:(j + 1) * 128, :]), [], [b_kst[s_]])
            dve(lambda e: e.tensor_copy(out=V[:, j, 0:nh * 65].rearrange("p (h d) -> p h d", d=65)[:, :, 0:64],
                                        in_=kst[s_][:, 0:ncols].rearrange("p (h d) -> p h d", d=64)), [b_kst[s_]], [b_Vj])
            if src_ki is not None:
                dmo(lambda e: e.dma_start(out=kis[:, :], in_=src_ki[j * 128:(j + 1) * 128, :]), [], [b_kis])
                act(lambda e: acp(e, out=kib[:, :], in_=kis[:, :]), [b_kis], [b_kib])
                pe(lambda e: e.transpose(out=tps2[0:32, 0:128], in_=kib[:, 0:32], identity=ident[:, :]), [b_kib, b_const], [b_tps2])
                act(lambda e: acp(e, out=kiT[0:32, j * 128:(j + 1) * 128], in_=tps2[0:32, 0:128]), [b_tps2], [b_kiT[j]])

        def CACHE_ALL():
            for j in range(4):
                cache_tile(ca_k, ca_v, j, kaT, b_kaT[j], va, b_va[j], 384, 6)
                yield
            for j in range(2):
                cache_tile(cm_k, cm_v, j, mkT, b_mk, mv, b_mv, 256, 4)
                yield
            for j in range(32):
                cache_tile(cb_k, cb_v, j, kbT, b_kbT[j], vb, b_vb[j], 384, 6, src_ki=cb_ki)
                yield

        cache_gen = CACHE_ALL() if DBG["sample"] else iter(())

        def cache_steps(n):
            for _ in range(n):
                next(cache_gen, None)

        halves = [(0, 768), (768, 1536), (1536, 2304), (2304, 3072), (3072, INC)]
        for k in range(8):
            for hf, (h0, h1) in enumerate(halves):
                dma(lambda e: e.dma_start(out=Isb[:, h0:h1], in_=w_in[k * 128:(k + 1) * 128, h0:h1]), [], [b_Ih[hf]])
                dve(lambda e: e.tensor_scalar(out=Mb[:, h0:h1], in0=Isb[:, h0:h1], scalar1=gcol[:, k:k + 1], scalar2=None,
                                              op0=ALU.mult), [b_Ih[hf], b_const], [b_Mh[hf]])
                for g, (c0, c1) in enumerate(SEGS):
                    if c0 >= h0 and c1 <= h1:
                        dma2(lambda e: e.dma_start(out=wbfg[g][:, k, :], in_=Mb[:, c0:c1]), [b_Mh[hf]], [b_wbf])
                cache_steps(1)
            dma(lambda e: e.dma_start(out=xt[:, 0:512], in_=w_mem[k * 128:(k + 1) * 128, :]), [], [b_xt])
            dve(lambda e: e.tensor_scalar(out=xn[:, 0:512], in0=xt[:, 0:512], scalar1=gcol[:, 8 + k:9 + k], scalar2=None,
                                          op0=ALU.mult), [b_xt, b_const], [b_xn])
            dma2(lambda e: e.dma_start(out=wmbf[:, k, :], in_=xn[:, 0:512]), [b_xn], [b_wmbf])
            dma(lambda e: e.dma_start(out=xres[:, :], in_=w_out[k * 128:(k + 1) * 128, :]), [], [b_xres])
            dve(lambda e: e.tensor_copy(out=wo[:, k, :], in_=xres[:, :]), [b_xres], [b_wo])
        cache_steps(64)
        dve(lambda e: e.memset(cntb[:, 0:1], 0.0), list(b_Ih) + list(b_Mh), [b_I, b_Mb, b_cnt])
        if DBG["setup"] <= 4:
            P.finish()
            return nc
        if DBG["setup"] <= 5:
            P.finish()
            return nc
        for _e in ("sp", "pe", "act", "dve", "pool"):
            P.dma_barrier(_e)
        wslot = [0]
        bg = []

        def tick():
            for g in list(bg):
                try:
                    next(g)
                except StopIteration:
                    bg.remove(g)

        def drain_bg():
            while bg:
                tick()

        def rmsnorm_T(nq, src_ap, split=False):
            dma(lambda e: e.dma_start(out=xt[0:nq, :], in_=src_ap), [], [b_xt])
            act(lambda e: e.activation(out=xn[0:nq, :], in_=xt[0:nq, :], func=AF.Square,
                                       accum_out=sm[0:nq, 0:1]), [b_xt], [b_xn, b_smn])
            dve(lambda e: e.tensor_scalar(out=sm[0:nq, 1:2], in0=sm[0:nq, 0:1], scalar1=1.0 / D, scalar2=1e-6,
                                          op0=ALU.mult, op1=ALU.add), [b_smn], [b_smn])
            act(lambda e: e.activation(out=sm[0:nq, 2:3], in_=sm[0:nq, 1:2], func=AF.Sqrt), [b_smn], [b_smn])
            dve(lambda e: e.reciprocal(out=sm[0:nq, 3:4], in_=sm[0:nq, 2:3]), [b_smn], [b_smn])
            dve(lambda e: e.tensor_scalar(out=xn[0:nq, :], in0=xt[0:nq, :], scalar1=sm[0:nq, 3:4], scalar2=None,
                                          op0=ALU.mult), [b_xt, b_smn], [b_xn])
            if split:
                return
            rmsnorm_T2(nq)

        def rmsnorm_T2(nq):
            for c in range(8):
                pe(lambda e: e.transpose(out=tps[:, c * 128:c * 128 + nq], in_=xn[0:nq, c * 128:(c + 1) * 128],
                                         identity=ident[0:nq, 0:nq]), [b_xn, b_const], [b_tps])
            act(lambda e: acp(e, out=xnT[:, :, 0:nq], in_=tps[:].rearrange("p (a b) -> p a b", b=128)[:, :, 0:nq]),
                [b_tps], [b_xnT])

        loaded = [None, None]

        def wload(s_, key, wsrc, b_wsrc, w):
            if loaded[s_] == key:
                return
            wv = wch[s_][:, 0:8 * w].rearrange("p (k w) -> p k w", w=w)
            dma(lambda e: e.dma_start(out=wv, in_=wsrc), [b_wsrc], [b_wch[s_]])
            loaded[s_] = key

        def project(nq, wsrc, b_wsrc, w, g=None):
            if g is None:
                s = wslot[0] % 2
                wslot[0] += 1
                wload(s, ("m", id(wsrc)), wsrc, b_wsrc, w)
            else:
                s = g % 2
                wload(s, ("g", g), wsrc, b_wsrc, w)
                g2 = (g + 1) % len(SEGS)
                wload(g2 % 2, ("g", g2), wbfg[g2], b_wbf, SEGS[g2][1] - SEGS[g2][0])
            wv = wch[s][:, 0:8 * w].rearrange("p (k w) -> p k w", w=w)
            for k in range(8):
                pe(lambda e: e.matmul(zps[s][0:nq, 0:w], lhsT=xnT[:, k, 0:nq], rhs=wv[:, k, :],
                                      start=(k == 0), stop=(k == 7)), [b_xnT, b_wch[s]], [b_zps[s]])
            flush_deferred()
            return zps[s], b_zps[s]

        def rope(nq, z, bz, H, half, tab, out_ap, b_o):
            w = H * 2 * half
            zv = z[0:nq, 0:w].rearrange("p (h t d) -> p h t d", h=H, t=2)
            cosb = tab[0:nq, 0:half].unsqueeze(1).unsqueeze(1).to_broadcast([nq, H, 2, half])
            sinb = tab[0:nq, half:2 * half].unsqueeze(1).to_broadcast([nq, H, half])
            nsinb = tab[0:nq, 2 * half:3 * half].unsqueeze(1).to_broadcast([nq, H, half])
            t1v = rt1[0:nq, 0:w].rearrange("p (h t d) -> p h t d", h=H, t=2)
            t2v = rt2[0:nq, 0:w].rearrange("p (h t d) -> p h t d", h=H, t=2)
            dve(lambda e: e.tensor_tensor(out=t1v, in0=zv, in1=cosb, op=ALU.mult), [bz, b_rope], [b_rt1])
            dve(lambda e: e.tensor_tensor(out=t2v[:, :, 0, :], in0=zv[:, :, 1, :], in1=nsinb, op=ALU.mult), [bz, b_rope], [b_rt2])
            dve(lambda e: e.tensor_tensor(out=t2v[:, :, 1, :], in0=zv[:, :, 0, :], in1=sinb, op=ALU.mult), [bz, b_rope], [b_rt2])
            wr = [b_o] if b_o is not b_rt1 else [b_rt1]
            dve(lambda e: e.tensor_tensor(out=out_ap, in0=rt1[0:nq, 0:w], in1=rt2[0:nq, 0:w], op=ALU.add),
                [b_rt1, b_rt2], wr)

        deferred = []
        later = []

        def run_later():
            while later:
                later.pop(0)()


        def flush_deferred():
            while deferred:
                deferred.pop(0)()

        def transpose_to(nq, src, b_src, ncols, dst_fn, b_dst, pt=tps, b_pt=None, defer=False):
            if defer:
                deferred.append(lambda: transpose_to(nq, src, b_src, ncols, dst_fn, b_dst, pt, b_pt))
                return
            b_pt = b_pt or b_tps
            nch = ncols // 128
            for c in range(nch):
                pe(lambda e: e.transpose(out=pt[:, c * 128:c * 128 + nq], in_=src[0:nq, c * 128:(c + 1) * 128],
                                         identity=ident[0:nq, 0:nq]), [b_src, b_const], [b_pt])
            for c in range(nch):
                act(lambda e: acp(e, out=dst_fn(c), in_=pt[:, c * 128:c * 128 + nq]), [b_pt], [b_dst])

        def attend(nq, par, nheads, qT, b_q, kT, b_k, V, b_v, tiles, bias_fn, mask_fn, use_mt, ocol0, blockbias=None):
            nt = len(tiles)
            nblk = (nt + 3) // 4
            units = [(bi, h) for bi in range(nblk) for h in range(nheads)]

            def scores(ui):
                bi, h = units[ui]
                blk = tiles[bi * 4:(bi + 1) * 4]
                c, pb = h // 2, (h % 2) * 64
                sp = ui % NSL
                spt = slot_ps[sp]
                for j, (kt, nk) in enumerate(blk):
                    last = bias_fn is None
                    if blockbias is not None:
                        pe(lambda e: e.matmul(spt[0:nk, j * 128:j * 128 + nq],
                                              lhsT=kT[pb:pb + 64, c, kt * 128:kt * 128 + nk],
                                              rhs=qT[pb:pb + 64, c, 0:nq], start=(j == 0), stop=False),
                           [b_k[kt], b_q], [slot_b[sp]])
                        continue
                    pe(lambda e: e.matmul(spt[0:nk, j * 128:j * 128 + nq],
                                          lhsT=kT[pb:pb + 64, c, kt * 128:kt * 128 + nk],
                                          rhs=qT[pb:pb + 64, c, 0:nq], start=True, stop=last),
                       [b_k[kt], b_q], [slot_b[sp]])
                    if bias_fn is not None:
                        m_ap = mask_fn(bi * 4 + j, nk, nq) if mask_fn is not None else None
                        pe(lambda e: e.matmul(spt[0:nk, j * 128:j * 128 + nq], lhsT=ident[0:nk, 0:nk],
                                              rhs=bias_fn(bi * 4 + j, h, nk, nq), start=False, stop=(m_ap is None)),
                           [b_const] + ([b_biasS] if nq < 128 else []), [slot_b[sp]])
                        if m_ap is not None:
                            pe(lambda e: e.matmul(spt[0:nk, j * 128:j * 128 + nq], lhsT=ident[0:nk, 0:nk],
                                                  rhs=m_ap, start=False, stop=True), [b_const], [slot_b[sp]])
                if blockbias is not None:
                    nb_ = len(blk)
                    pe(lambda e: e.matmul(spt[:, 0:nb_ * 128].rearrange("p (a b) -> p a b", b=128), lhsT=ident[:, :],
                                          rhs=blockbias(bi * 4, nb_, h), start=False, stop=True),
                       [b_const], [slot_b[sp]])
                nfull = sum(1 for (_, nk) in blk if nk == 128)
                kt0 = blk[0][0]
                if nfull > 0:
                    act(lambda e: e.activation(out=PTs[sp][:, 0:nfull * 128].rearrange("p (a b) -> p a b", b=128)[:, :, 0:nq],
                                               in_=spt[:, 0:nfull * 128].rearrange("p (a b) -> p a b", b=128)[:, :, 0:nq],
                                               func=AF.Exp), [slot_b[sp]], [PTb[sp]])
                    if use_mt:
                        dve(lambda e: e.tensor_tensor(
                            out=PTs[sp][:, 0:nfull * 128].rearrange("p (a b) -> p a b", b=128)[:, :, 0:nq],
                            in0=PTs[sp][:, 0:nfull * 128].rearrange("p (a b) -> p a b", b=128)[:, :, 0:nq],
                            in1=MT[:, kt0:kt0 + nfull, 0:nq], op=ALU.mult), [PTb[sp], b_MT], [PTb[sp]])
                for j, (kt, nk) in enumerate(blk):
                    if nk < 128:
                        act(lambda e: e.activation(out=PTs[sp][0:nk, j * 128:j * 128 + nq],
                                                   in_=spt[0:nk, j * 128:j * 128 + nq], func=AF.Exp),
                            [slot_b[sp]], [PTb[sp]])
                        if use_mt:
                            dve(lambda e: e.tensor_tensor(out=PTs[sp][0:nk, j * 128:j * 128 + nq],
                                                          in0=PTs[sp][0:nk, j * 128:j * 128 + nq],
                                                          in1=MT[0:nk, kt, 0:nq], op=ALU.mult),
                                [PTb[sp], b_MT], [PTb[sp]])

            def pv(ui):
                bi, h = units[ui]
                blk = tiles[bi * 4:(bi + 1) * 4]
                sp = ui % NSL
                for j, (kt, nk) in enumerate(blk):
                    gi = bi * 4 + j
                    pe(lambda e: e.matmul(ops[0:nq, h * 65:(h + 1) * 65], lhsT=PTs[sp][0:nk, j * 128:j * 128 + nq],
                                          rhs=V[0:nk, kt, h * 65:(h + 1) * 65], start=(ui == 0 and j == 0),
                                          stop=(ui == len(units) - 1 and gi == nt - 1)),
                       [PTb[sp], b_v[kt]], [b_ops])
                tick()

            NSL = 3 if nq == 128 else 2
            LAG = NSL - 1
            slot_ps = [sps[:, 0:512], sps[:, 512:1024], ips[:, 0:512]]
            slot_b = [b_sps[0], b_sps[1], b_ips]
            PTs = [PT[0], PT[1], biasS[:].rearrange("p a b c -> p (a b c)")[:, 0:640]]
            PTb = [b_PT[0], b_PT[1], b_biasS]
            for ui in range(len(units)):
                scores(ui)
                tick()
                if ui == 2:
                    run_later()
                if ui >= LAG:
                    pv(ui - LAG)
            for ui in range(max(0, len(units) - LAG), len(units)):
                pv(ui)
            ov = ops[0:nq, 0:nheads * 65].rearrange("p (h d) -> p h d", d=65)
            dve(lambda e: e.reciprocal(out=rden[0:nq, 0:nheads], in_=ov[:, :, 64]), [b_ops], [b_rden])
            dve(lambda e: e.tensor_tensor(out=otmp[0:nq, 0:nheads * 64].rearrange("p (h d) -> p h d", d=64),
                                          in0=ov[:, :, 0:64],
                                          in1=rden[0:nq, 0:nheads].unsqueeze(2).to_broadcast([nq, nheads, 64]),
                                          op=ALU.mult), [b_ops, b_rden], [b_otmp])
            dve(lambda e: e.tensor_tensor(out=oall[par][0:nq, ocol0:ocol0 + nheads * 64], in0=otmp[0:nq, 0:nheads * 64],
                                          in1=gates[par][0:nq, ocol0:ocol0 + nheads * 64], op=ALU.mult),
                [b_otmp, b_gates[par]], [b_oall[par]])

        class Tile:
            pass

        def S1(t):
            nq, par, outs, kt_a, kt_b = t.nq, t.par, t.outs, t.kt_a, t.kt_b
            rmsnorm_T(nq, t.x_ap, split=True)
            dmo(lambda e: e.dma_start(out=ropeT[0:nq, :], in_=c_ropeT[t.pos0:t.pos0 + nq, :]), [], [b_rope])
            dmo(lambda e: e.dma_start(out=ropeI[0:nq, :], in_=c_ropeI[t.pos0:t.pos0 + nq, :]), [], [b_rope])
            yield
            yield
            yield
            rmsnorm_T2(nq)
            kcnt = [0]

            def stage_out(z, bz, w, dram_ap, src_is_sb=None):
                s = kcnt[0] % 2
                kcnt[0] += 1
                if src_is_sb is None:
                    act(lambda e: acp(e, out=kst[s][0:nq, 0:w], in_=z[0:nq, 0:w]), [bz], [b_kst[s]])
                dmo(lambda e: e.dma_start(out=dram_ap, in_=kst[s][0:nq, 0:w]), [b_kst[s]], [])
                return s

            z, bz = project(nq, wbfg[0], b_wbf, SEGS[0][1] - SEGS[0][0], g=0)
            act(lambda e: e.mul(out=qtok[0:nq, :], in_=z[0:nq, 0:384], mul=0.125), [bz], [b_qtok])
            transpose_to(nq, qtok, b_qtok, 384, lambda c: qaT[par][:, c, 0:nq], b_qaT[par], defer=True)
            yield
            z, bz = project(nq, wbfg[1], b_wbf, SEGS[1][1] - SEGS[1][0], g=1)
            act(lambda e: acp(e, out=ktok[0:nq, :], in_=z[0:nq, 0:384]), [bz], [b_ktok])
            if outs.get("ak") is not None:
                stage_out(z, bz, 384, outs["ak"])
            transpose_to(nq, ktok, b_ktok, 384, lambda c: kaT[:, c, kt_a * 128:kt_a * 128 + nq], b_kaT[kt_a], defer=True)
            yield
            z, bz = project(nq, wbfg[2], b_wbf, SEGS[2][1] - SEGS[2][0], g=2)
            dve(lambda e: e.tensor_copy(out=va[0:nq, kt_a, :].rearrange("p (h d) -> p h d", d=65)[:, :, 0:64],
                                        in_=z[0:nq, 0:384].rearrange("p (h d) -> p h d", d=64)), [bz], [b_va[kt_a]])
            if outs.get("av") is not None:
                stage_out(z, bz, 384, outs["av"])
            yield
            while not t.go:
                yield
            z, bz = project(nq, wbfg[3], b_wbf, SEGS[3][1] - SEGS[3][0], g=3)
            act(lambda e: e.activation(out=gates[par][0:nq, 0:384], in_=z[0:nq, 0:384], func=AF.Silu), [bz], [b_gates[par]])
            yield
            z, bz = project(nq, wbfg[4], b_wbf, SEGS[4][1] - SEGS[4][0], g=4)
            rope(nq, z, bz, 6, 32, ropeT, rt1[0:nq, 0:384], b_rt1)
            act(lambda e: e.mul(out=qtok[0:nq, :], in_=rt1[0:nq, 0:384], mul=0.125), [b_rt1], [b_qtok])
            transpose_to(nq, qtok, b_qtok, 384, lambda c: qbT[par][:, c, 0:nq], b_qbT[par], defer=True)
            yield
            z, bz = project(nq, wbfg[5], b_wbf, SEGS[5][1] - SEGS[5][0], g=5)
            s = kcnt[0] % 2
            rope(nq, z, bz, 6, 32, ropeT, kst[s][0:nq, 0:384], b_kst[s])
            act(lambda e: acp(e, out=ktok[0:nq, :], in_=kst[s][0:nq, 0:384]), [b_kst[s]], [b_ktok])
            stage_out(None, None, 384, outs["bk"], src_is_sb=True)
            transpose_to(nq, ktok, b_ktok, 384, lambda c: kbT[:, c, kt_b * 128:kt_b * 128 + nq], b_kbT[kt_b], defer=True)
            yield
            z, bz = project(nq, wbfg[6], b_wbf, SEGS[6][1] - SEGS[6][0], g=6)
            dve(lambda e: e.tensor_copy(out=vb[0:nq, kt_b, :].rearrange("p (h d) -> p h d", d=65)[:, :, 0:64],
                                        in_=z[0:nq, 0:384].rearrange("p (h d) -> p h d", d=64)), [bz], [b_vb[kt_b]])
            stage_out(z, bz, 384, outs["bv"])
            yield
            z, bz = project(nq, wbfg[7], b_wbf, SEGS[7][1] - SEGS[7][0], g=7)
            act(lambda e: e.activation(out=gates[par][0:nq, 384:768], in_=z[0:nq, 0:384], func=AF.Silu), [bz], [b_gates[par]])
            yield
            z, bz = project(nq, wbfg[8], b_wbf, SEGS[8][1] - SEGS[8][0], g=8)
            act(lambda e: e.mul(out=qmtok[0:nq, 0:256], in_=z[0:nq, 0:256], mul=0.125), [bz], [b_qmtok])
            act(lambda e: e.activation(out=gates[par][0:nq, 768:1024], in_=z[0:nq, 256:512], func=AF.Silu), [bz], [b_gates[par]])
            transpose_to(nq, qmtok, b_qmtok, 256, lambda c: qmT[par][:, c, 0:nq], b_qmT[par], defer=True)
            yield
            z, bz = project(nq, wbfg[9], b_wbf, SEGS[9][1] - SEGS[9][0], g=9)
            rope(nq, z, bz, 8, 16, ropeI, rt1[0:nq, 0:256], b_rt1)
            act(lambda e: acp(e, out=wis[0:nq, :], in_=z[0:nq, 288:296]), [bz], [b_wis])
            zk = z[0:nq, 256:288]
            dve(lambda e: e.tensor_tensor(out=rt1[0:nq, 320:352].rearrange("p (t d) -> p t d", t=2),
                                          in0=zk.rearrange("p (t d) -> p t d", t=2),
                                          in1=ropeI[0:nq, 0:16].unsqueeze(1).to_broadcast([nq, 2, 16]), op=ALU.mult),
                [bz, b_rope], [b_rt1])
            dve(lambda e: e.tensor_tensor(out=rt2[0:nq, 320:336], in0=zk[:, 16:32], in1=ropeI[0:nq, 32:48], op=ALU.mult),
                [bz, b_rope], [b_rt2])
            dve(lambda e: e.tensor_tensor(out=rt2[0:nq, 336:352], in0=zk[:, 0:16], in1=ropeI[0:nq, 16:32], op=ALU.mult),
                [bz, b_rope], [b_rt2])
            dve(lambda e: e.tensor_tensor(out=kis[0:nq, :], in0=rt1[0:nq, 320:352], in1=rt2[0:nq, 320:352], op=ALU.add),
                [b_rt1, b_rt2], [b_kis])
            dmo(lambda e: e.dma_start(out=outs["bi"], in_=kis[0:nq, :]), [b_kis], [])
            act(lambda e: acp(e, out=kib[0:nq, :], in_=kis[0:nq, :]), [b_kis], [b_kib])
            pe(lambda e: e.transpose(out=tps[0:32, 0:nq], in_=kib[0:nq, 0:32], identity=ident[0:nq, 0:nq]),
               [b_kib, b_const], [b_tps])
            act(lambda e: acp(e, out=kiT[0:32, kt_b * 128:kt_b * 128 + nq], in_=tps[0:32, 0:nq]), [b_tps], [b_kiT[kt_b]])
            yield
            act(lambda e: e.activation(out=sgn[0:nq, :], in_=wis[0:nq, :], func=AF.Sign), [b_wis], [b_sgn])
            act(lambda e: e.activation(out=absw[0:nq, :], in_=wis[0:nq, :], func=AF.Abs, scale=0.0625), [b_wis], [b_absw])
            dve(lambda e: e.tensor_tensor(out=qib[0:nq, :].rearrange("p (h d) -> p h d", d=32),
                                          in0=rt1[0:nq, 0:256].rearrange("p (h d) -> p h d", d=32),
                                          in1=absw[0:nq, :].unsqueeze(2).to_broadcast([nq, 8, 32]), op=ALU.mult),
                [b_rt1, b_absw], [b_qib])
            for h in range(8):
                pe(lambda e: e.transpose(out=tps[0:32, h * 128:h * 128 + nq], in_=qib[0:nq, h * 32:(h + 1) * 32],
                                         identity=ident[0:nq, 0:nq]), [b_qib, b_const], [b_tps])
            act(lambda e: acp(e, out=qid[0:32, 0:nq * 8].rearrange("p (t h) -> p h t", h=8),
                              in_=tps[0:32, :].rearrange("p (h t) -> p h t", t=128)[:, :, 0:nq]), [b_tps], [b_qid])
            pool(lambda e: e.tensor_tensor(out=Et[0:nq, :].rearrange("p (t h) -> p t h", h=8),
                                           in0=dmask[0:nq, :].rearrange("p (t h) -> p t h", h=8),
                                           in1=sgn[0:nq, :].unsqueeze(1).to_broadcast([nq, 16, 8]), op=ALU.mult),
                 [b_const, b_sgn], [b_Et])
            pe(lambda e: e.transpose(out=tps2[:, 0:nq], in_=Et[0:nq, :], identity=ident[0:nq, 0:nq]),
               [b_Et, b_const], [b_tps2])
            ng = nq // 16
            for g in range(ng):
                act(lambda e: acp(e, out=Z[:, g, 16 * g:16 * g + 16], in_=tps2[:, 16 * g:16 * g + 16]), [b_tps2], [b_Z])
            flush_deferred()

        def IDX(t):
            nq = t.nq
            ng = nq // 16
            nkeys = t.nkeys
            units = []
            k0 = 0
            while k0 < nkeys:
                k1 = min(k0 + 512, nkeys)
                for g in range(ng):
                    units.append((k0, k1, g))
                k0 = k1

            if nq == 128:
                dsl = [sps[:, 0:512], sps[:, 512:1024], zps[0], zps[1]]
                dsb = [b_sps[0], b_sps[1], b_zps[0], b_zps[1]]
                rsl = [PT[0], PT[1], biasS[:].rearrange("p a b c -> p (a b c)")[:, 0:512],
                       oT[:].rearrange("p a b -> p (a b)")[:, 0:512]]
                rsb = [b_PT[0], b_PT[1], b_biasS, b_oT]
            else:
                dsl = [sps[:, 0:512], sps[:, 512:1024]]
                dsb = [b_sps[0], b_sps[1]]
                rsl = [PT[0], PT[1]]
                rsb = [b_PT[0], b_PT[1]]
            NS = len(dsl)

            def stage1(ui):
                k0, k1, g = units[ui]
                ncol = k1 - k0
                kts = list(range(k0 // 128, (k1 + 127) // 128))
                sp = ui % NS
                spt = dsl[sp]
                pe(lambda e: e.matmul(spt[:, 0:ncol], lhsT=qid[0:32, g * 128:(g + 1) * 128], rhs=kiT[0:32, k0:k1],
                                      start=True, stop=True), [b_qid] + [b_kiT[x] for x in kts], [dsb[sp]])
                if ui % 2 == 0:
                    act(lambda e: e.activation(out=rsl[sp][:, 0:ncol], in_=spt[:, 0:ncol], func=AF.Relu),
                        [dsb[sp]], [rsb[sp]])
                else:
                    dve(lambda e: e.tensor_scalar(out=rsl[sp][:, 0:ncol], in0=spt[:, 0:ncol], scalar1=0.0, scalar2=None,
                                                  op0=ALU.max), [dsb[sp]], [rsb[sp]])

            def stage2(ui):
                k0, k1, g = units[ui]
                ncol = k1 - k0
                sp = ui % NS
                pe(lambda e: e.matmul(ips[0:nq, 0:ncol], lhsT=Z[:, g, 0:nq], rhs=rsl[sp][:, 0:ncol],
                                      start=(g == 0), stop=(g == ng - 1)), [b_Z, rsb[sp]], [b_ips])
                if g == ng - 1:
                    act(lambda e: acp(e, out=Isb[0:nq, k0:k1], in_=ips[0:nq, 0:ncol]), [b_ips], [b_I])

            LG = NS - 1
            for ui in range(len(units)):
                stage1(ui)
                if ui >= LG:
                    stage2(ui - LG)
            for ui in range(max(0, len(units) - LG), len(units)):
                stage2(ui)
            if t.corner is not None:
                dve(lambda e: e.memset(Isb[0:64, t.corner:t.corner + 64], NBIG), [], [b_I])

        def THR(t):
            nq, nkeys, n_safe = t.nq, t.nkeys, t.n_safe
            THRc = sm[0:nq, 26:27]
            if not t.bisect:
                dve(lambda e: e.memset(THRc, -1.0e29), [], [b_smt])
                yield
            else:
                dve(lambda e: e.tensor_reduce(out=sm[0:nq, 20:21], in_=Isb[0:nq, 0:nkeys], axis=AX.X, op=ALU.max), [b_I], [b_smt])
                yield
                dve(lambda e: e.tensor_reduce(out=sm[0:nq, 21:22], in_=Isb[0:nq, 0:n_safe], axis=AX.X, op=ALU.min), [b_I], [b_smt])
                yield
                dve(lambda e: e.tensor_tensor(out=sm[0:nq, 22:23], in0=sm[0:nq, 20:21], in1=sm[0:nq, 21:22], op=ALU.subtract), [b_smt], [b_smt])
                yield
                dve(lambda e: e.tensor_scalar(out=sm[0:nq, 23:24], in0=sm[0:nq, 22:23], scalar1=1.0001, scalar2=1e-20,
                                              op0=ALU.mult, op1=ALU.add), [b_smt], [b_smt])
                yield
                dve(lambda e: e.tensor_scalar(out=halfs[0:nq, :], in0=pow2[0:nq, :], scalar1=sm[0:nq, 23:24], scalar2=None,
                                              op0=ALU.mult), [b_smt, b_const], [b_halfs])
                yield
                dve(lambda e: e.memset(cntb[0:nq, :], 0.0), [], [b_cnt])
                yield
                MID = sm[0:nq, 24:25]
                dve(lambda e: e.tensor_tensor(out=MID, in0=sm[0:nq, 21:22], in1=halfs[0:nq, 1:2], op=ALU.add), [b_smt, b_halfs], [b_smt])
                yield
                for k in range(1, KIT + 1):
                    dve(lambda e: e.tensor_scalar(out=Mb[0:nq, 0:nkeys], in0=Isb[0:nq, 0:nkeys], scalar1=MID, scalar2=0.0,
                                                  op0=ALU.is_ge, op1=ALU.add, accum_out=cntb[0:nq, k:k + 1]),
                        [b_I, b_smt], [b_Mb, b_cnt])
                    yield
                    if k < KIT:
                        dve(lambda e: e.tensor_scalar(out=sm[0:nq, 25:26], in0=cntb[0:nq, k:k + 1], scalar1=255.5,
                                                      scalar2=halfs[0:nq, k:k + 1], op0=ALU.is_ge, op1=ALU.mult),
                            [b_cnt, b_halfs], [b_smt])
                        yield
                        dve(lambda e: e.scalar_tensor_tensor(out=MID, in0=sm[0:nq, 25:26], scalar=halfs[0:nq, k + 1:k + 2],
                                                             in1=MID, op0=ALU.subtract, op1=ALU.add),
                            [b_smt, b_halfs], [b_smt])
                        yield
                    else:
                        dve(lambda e: e.tensor_scalar(out=sm[0:nq, 25:26], in0=cntb[0:nq, k:k + 1], scalar1=255.5,
                                                      scalar2=halfs[0:nq, k:k + 1], op0=ALU.is_lt, op1=ALU.mult),
                            [b_cnt, b_halfs], [b_smt])
                        yield
                        dve(lambda e: e.tensor_tensor(out=THRc, in0=MID, in1=sm[0:nq, 25:26], op=ALU.subtract), [b_smt], [b_smt])
                        yield
            dve(lambda e: e.tensor_scalar(out=Mb[0:nq, 0:nkeys], in0=Isb[0:nq, 0:nkeys], scalar1=THRc, scalar2=0.0,
                                          op0=ALU.is_ge, op1=ALU.add), [b_I, b_smt], [b_Mb])
            yield

        def AM(t):
            attend(t.nq, t.par, 6, qaT[t.par], b_qaT[t.par], kaT, b_kaT, va, b_va, t.a_tiles, t.a_bias, t.a_mask, False, 0,
                   blockbias=t.a_blockbias)
            attend(t.nq, t.par, 4, qmT[t.par], b_qmT[t.par], mkT, [b_mk, b_mk], mv, [b_mv, b_mv], [(0, 128), (1, 128)], None, None, False, 768)

        def MTB(t):
            nq = t.nq
            b_tiles = t.b_tiles
            for j0 in range(0, len(b_tiles), 8):
                js = b_tiles[j0:j0 + 8]
                for jj, (kt, nk) in enumerate(js):
                    pe(lambda e: e.transpose(out=tps2[0:nk, jj * 128:jj * 128 + nq], in_=Mb[0:nq, kt * 128:kt * 128 + nk],
                                             identity=ident[0:nq, 0:nq]), [b_Mb, b_const], [b_tps2])
                nfull = sum(1 for (_, nk) in js if nk == 128)
                if nfull > 0:
                    dve(lambda e: e.tensor_copy(out=MT[:, js[0][0]:js[0][0] + nfull, 0:nq],
                                                in_=tps2[:, 0:nfull * 128].rearrange("p (a b) -> p a b", b=128)[:, :, 0:nq]),
                        [b_tps2], [b_MT])
                for jj, (kt, nk) in enumerate(js):
                    if nk < 128:
                        dve(lambda e: e.tensor_copy(out=MT[0:nk, kt, 0:nq], in_=tps2[0:nk, jj * 128:jj * 128 + nq]), [b_tps2], [b_MT])

        def BOUT2(t):
            nq, par = t.nq, t.par
            b_tiles = t.b_tiles
            dma(lambda e: e.dma_start(out=xres[0:nq, :], in_=t.x_ap), [], [b_xres])
            attend(nq, par, 6, qbT[par], b_qbT[par], kbT, b_kbT, vb, b_vb, b_tiles, None, None, True, 384)
            later.append(lambda: OUTP(t))

        def OUTP(t):
            nq, par = t.nq, t.par
            for c in range(8):
                pe(lambda e: e.transpose(out=tps[:, c * 128:c * 128 + nq], in_=oall[par][0:nq, c * 128:(c + 1) * 128],
                                         identity=ident[0:nq, 0:nq]), [b_oall[par], b_const], [b_tps])
            act(lambda e: acp(e, out=oT[:, :, 0:nq], in_=tps[:].rearrange("p (a b) -> p a b", b=128)[:, :, 0:nq]),
                [b_tps], [b_oT])
            for cg in range(2):
                for k in range(8):
                    pe(lambda e: e.matmul(zps[cg][0:nq, :], lhsT=oT[:, k, 0:nq], rhs=wo[:, k, cg * 512:(cg + 1) * 512],
                                          start=(k == 0), stop=(k == 7)), [b_oT, b_wo], [b_zps[cg]])
                dve(lambda e: e.tensor_tensor(out=xres[0:nq, cg * 512:(cg + 1) * 512], in0=zps[cg][0:nq, :],
                                              in1=xres[0:nq, cg * 512:(cg + 1) * 512], op=ALU.add), [b_zps[cg], b_xres], [b_xres])
            act(lambda e: e.activation(out=oall[par][0:nq, :], in_=xres[0:nq, :], func=AF.Square, accum_out=sm[0:nq, 12:13]),
                [b_xres], [b_oall[par], b_smo])
            dve(lambda e: e.tensor_scalar(out=sm[0:nq, 13:14], in0=sm[0:nq, 12:13], scalar1=1.0 / D, scalar2=1e-6,
                                          op0=ALU.mult, op1=ALU.add), [b_smo], [b_smo])
            act(lambda e: e.activation(out=sm[0:nq, 14:15], in_=sm[0:nq, 13:14], func=AF.Sqrt), [b_smo], [b_smo])
            dve(lambda e: e.reciprocal(out=sm[0:nq, 15:16], in_=sm[0:nq, 14:15]), [b_smo], [b_smo])
            dve(lambda e: e.scalar_tensor_tensor(out=xres[0:nq, :], in0=xres[0:nq, :], scalar=sm[0:nq, 15:16], in1=gfb[0:nq, :],
                                                 op0=ALU.mult, op1=ALU.mult), [b_xres, b_smo, b_const], [b_xres])
            dmo(lambda e: e.dma_start(out=t.outs["y"], in_=xres[0:nq, :]), [b_xres], [])

        def load_cache_kv(src_k, src_v, ntile, kT, b_kT, V, b_V, ncols, nh):
            for j in range(ntile):
                s = j % 2
                dma(lambda e: e.dma_start(out=kst[s][:, 0:ncols], in_=src_k[j * 128:(j + 1) * 128, :]), [], [b_kst[s]])
                act(lambda e: acp(e, out=ktok[:, 0:ncols], in_=kst[s][:, 0:ncols]), [b_kst[s]], [b_ktok])
                transpose_to(128, ktok, b_ktok, ncols, lambda c: kT[:, c, j * 128:(j + 1) * 128], b_kT[j])
                dma(lambda e: e.dma_start(out=kst[s][:, 0:ncols], in_=src_v[j * 128:(j + 1) * 128, :]), [], [b_kst[s]])
                dve(lambda e: e.tensor_copy(out=V[:, j, 0:nh * 65].rearrange("p (h d) -> p h d", d=65)[:, :, 0:64],
                                            in_=kst[s][:, 0:ncols].rearrange("p (h d) -> p h d", d=64)), [b_kst[s]], [b_V[j]])

        def cache_b_tile(j):
            s_ = j % 2
            flush_deferred()
            dma(lambda e: e.dma_start(out=kst[s_][:, 0:384], in_=cb_k[j * 128:(j + 1) * 128, :]), [], [b_kst[s_]])
            act(lambda e: acp(e, out=ktok[:, 0:384], in_=kst[s_][:, 0:384]), [b_kst[s_]], [b_ktok])
            transpose_to(128, ktok, b_ktok, 384, lambda c: kbT[:, c, j * 128:(j + 1) * 128], b_kbT[j])
            dma(lambda e: e.dma_start(out=kst[s_][:, 0:384], in_=cb_v[j * 128:(j + 1) * 128, :]), [], [b_kst[s_]])
            dve(lambda e: e.tensor_copy(out=vb[:, j, 0:390].rearrange("p (h d) -> p h d", d=65)[:, :, 0:64],
                                        in_=kst[s_][:, 0:384].rearrange("p (h d) -> p h d", d=64)), [b_kst[s_]], [b_vb[j]])
            dma(lambda e: e.dma_start(out=kis[:, :], in_=cb_ki[j * 128:(j + 1) * 128, :]), [], [b_kis])
            act(lambda e: acp(e, out=kib[:, :], in_=kis[:, :]), [b_kis], [b_kib])
            pe(lambda e: e.transpose(out=tps[0:32, 0:128], in_=kib[:, 0:32], identity=ident[:, :]), [b_kib, b_const], [b_tps])
            act(lambda e: acp(e, out=kiT[0:32, j * 128:(j + 1) * 128], in_=tps[0:32, 0:128]), [b_tps], [b_kiT[j]])

        def CACHE_HI():
            for j in range(16, 32):
                cache_b_tile(j)
                yield
                yield
                yield

        def run_pipeline(tiles):
            if not tiles:
                return
            for t_ in tiles:
                t_.go = False
            tiles[0].go = True
            for _ in S1(tiles[0]):
                pass
            IDX(tiles[0])
            prev = None
            for n, t in enumerate(tiles):
                bg.append(THR(t))
                if n + 1 < len(tiles):
                    bg.append(S1(tiles[n + 1]))
                if prev is not None:
                    BOUT2(prev)
                if n + 1 < len(tiles):
                    tiles[n + 1].go = True
                AM(t)
                drain_bg()
                run_later()
                if n + 1 < len(tiles):
                    IDX(tiles[n + 1])
                MTB(t)
                prev = t
            BOUT2(prev)
            run_later()

        if DBG["sample"]:
            t = Tile()
            t.nq = T; t.par = 0; t.x_ap = xs[:, :]; t.pos0 = PAST; t.kt_a = 4; t.kt_b = 32
            t.a_tiles = [(0, 128), (1, 128), (2, 128), (3, 128), (4, T)]
            t.a_bias = lambda ti, h, nk, nq: biasS[0:nk, ti, h, 0:nq]
            t.a_mask = None
            t.a_blockbias = None
            t.b_tiles = [(j, 128) for j in range(32)] + [(32, T)]
            t.nkeys = PAST + T; t.corner = None; t.n_safe = PAST + T; t.bisect = True
            t.outs = {"y": y_s[:, :], "ak": aks[:, :], "av": avs[:, :], "bk": bks[:, :], "bv": bvs[:, :], "bi": bis[:, :]}
            run_pipeline([t])
        for b in range(DBG["nb"]):
            for mt in range(2 if DBG["mem"] else 0):
                rmsnorm_T(128, memp[b, mt * 128:(mt + 1) * 128, :])
                z, bz = project(128, wmbf, b_wmbf, 512)
                act(lambda e: acp(e, out=kst[0][:, 0:256], in_=z[:, 0:256]), [bz], [b_kst[0]])
                dmo(lambda e: e.dma_start(out=mkp[b, mt * 128:(mt + 1) * 128, :], in_=kst[0][:, 0:256]), [b_kst[0]], [])
                act(lambda e: acp(e, out=kst[1][:, 0:256], in_=z[:, 256:512]), [bz], [b_kst[1]])
                dmo(lambda e: e.dma_start(out=mvp[b, mt * 128:(mt + 1) * 128, :], in_=kst[1][:, 0:256]), [b_kst[1]], [])
                act(lambda e: acp(e, out=ktok[:, 0:256], in_=z[:, 0:256]), [bz], [b_ktok])
                transpose_to(128, ktok, b_ktok, 256, lambda c: mkT[:, c, mt * 128:(mt + 1) * 128], b_mk)
                dve(lambda e: e.tensor_copy(out=mv[:, mt, :].rearrange("p (h d) -> p h d", d=65)[:, :, 0:64],
                                            in_=z[:, 256:512].rearrange("p (h d) -> p h d", d=64)), [bz], [b_mv])
            tl = []
            for i in range(DBG["nt"]):
                t = Tile()
                t.nq = 128; t.par = i % 2; t.x_ap = xp[b, i * 128:(i + 1) * 128, :]; t.pos0 = i * 128
                t.kt_a = i; t.kt_b = i
                t.a_tiles = [(kt, 128) for kt in range(max(0, i - 4), i + 1)]
                kt0 = t.a_tiles[0][0]
                nta = len(t.a_tiles)
                t.a_bias = None
                t.a_mask = None
                t.a_blockbias = (lambda g0, nb_, h, nta=nta: biasP[:, h, 5 - nta + g0:5 - nta + g0 + nb_, :])
                t.b_tiles = [(j, 128) for j in range(i + 1)]
                t.nkeys = (i + 1) * 128
                t.corner = i * 128 + 64; t.n_safe = i * 128 + 64; t.bisect = i >= 2
                t.outs = {"y": y_p[b, i * 128:(i + 1) * 128, :], "bk": bkp[b, i * 128:(i + 1) * 128, :],
                          "bv": bvp[b, i * 128:(i + 1) * 128, :], "bi": bip[b, i * 128:(i + 1) * 128, :]}
                if i >= 12:
                    t.outs["ak"] = akp[b, (i - 12) * 128:(i - 11) * 128, :]
                    t.outs["av"] = avp[b, (i - 12) * 128:(i - 11) * 128, :]
                tl.append(t)
            run_pipeline(tl)

        P.finish()
        print("instructions:", P.nins, "sbuf remaining:", nc.sbuf_bytes_remaining)
    return nc


def _consts(rel_bias):
    ident = np.eye(128, dtype=np.float32).astype(ml_dtypes.bfloat16)
    dm = np.zeros((128, 16, 8), np.float32)
    for p in range(128):
        dm[p, p % 16, :] = 1.0
    dmask = dm.reshape(128, 128).astype(ml_dtypes.bfloat16)
    pow2 = np.tile((2.0 ** -np.arange(KIT + 1, dtype=np.float64)).astype(np.float32)[None, :], (128, 1))
    mA = np.zeros((128, 5, 128), np.float32)
    mA[64:128, 4, 0:64] = NEG
    mA[0:64, 0, 64:128] = NEG
    maskA = mA.reshape(128, 640).astype(ml_dtypes.bfloat16)

    def rope_tab(d, npos):
        half = d // 2
        inv = (np.float32(10000.0) ** (-np.arange(half, dtype=np.float32) * np.float32(2.0) / np.float32(d))).astype(np.float32)
        ang = np.arange(npos, dtype=np.float32)[:, None] * inv[None, :]
        c = np.cos(ang).astype(np.float32)
        s = np.sin(ang).astype(np.float32)
        return np.ascontiguousarray(np.concatenate([c, s, -s], axis=1).astype(np.float32))

    ropeT = rope_tab(64, PAST + 128)
    ropeI = rope_tab(32, PAST + 128)
    tab = np.asarray(rel_bias, np.float32)
    a = np.arange(128)[:, None, None]
    m = np.arange(5)[None, :, None]
    bq = np.arange(128)[None, None, :]
    idx = np.clip(128 * (4 - m) + bq - a, -256, 256) + 256
    biasP = np.ascontiguousarray(np.transpose(tab[:, idx], (1, 0, 2, 3))).reshape(128, 6 * 5 * 128)
    key = np.arange(640).reshape(5, 128)
    tq = np.arange(32)
    idx2 = np.clip(512 + tq[None, None, :] - key[:, :, None], -256, 256) + 256
    bs = tab[:, idx2]
    biasS = np.ascontiguousarray(np.transpose(bs, (2, 1, 0, 3))).reshape(128, 5 * 6 * 32)
    return dict(c_ident=ident, c_dmask=dmask, c_pow2=pow2, c_maskA=maskA, c_ropeT=ropeT, c_ropeI=ropeI,
                c_biasP=biasP.astype(np.float32), c_biasS=biasS.astype(np.float32))


_NC = None


def kernel(x_prompt, x_sample, mem_prompt, cache_a_k, cache_a_v, cache_b_k, cache_b_v, cache_b_kidx,
           cache_mem_k, cache_mem_v, norm_mix_g, w_in, rel_bias_a, norm_mem_g, w_mem_kv, w_out, norm_final_g):
    global _NC
    f = lambda a: np.ascontiguousarray(np.asarray(a, dtype=np.float32))
    x_prompt, x_sample, mem_prompt = f(x_prompt), f(x_sample), f(mem_prompt)
    consts = _consts(np.asarray(rel_bias_a)[0])
    if _NC is None:
        _NC = build_nc()
    nc = _NC
    in_maps = []
    for c in range(8):
        m = dict(consts)
        m.update(
            xp=x_prompt[2 * c:2 * c + 2], xs=x_sample[c], memp=mem_prompt[2 * c:2 * c + 2],
            ca_k=f(cache_a_k)[0, c].reshape(512, 384), ca_v=f(cache_a_v)[0, c].reshape(512, 384),
            cb_k=f(cache_b_k)[0, c].reshape(PAST, 384), cb_v=f(cache_b_v)[0, c].reshape(PAST, 384),
            cb_ki=f(cache_b_kidx)[0, c], cm_k=f(cache_mem_k)[0, c].reshape(256, 256),
            cm_v=f(cache_mem_v)[0, c].reshape(256, 256),
            g_mix=f(norm_mix_g)[0], w_in=f(w_in)[0], g_mem=f(norm_mem_g)[0], w_mem=f(w_mem_kv)[0],
            w_out=f(w_out)[0], g_fin=f(norm_final_g),
        )
        in_maps.append({k: np.ascontiguousarray(v) for k, v in m.items()})
    res = run_bass_kernel_spmd(nc, in_maps, core_ids=list(range(8)))
    R = res.results
    cat = lambda k: np.concatenate([np.asarray(r[k]) for r in R], axis=0)
    stk = lambda k: np.stack([np.asarray(r[k]) for r in R], axis=0)
    y_prompt = cat("y_p").reshape(16, S, D)
    y_sample = stk("y_s").reshape(8, T, D)
    out = (
        y_prompt, y_sample,
        cat("akp").reshape(1, 16, 512, 6, 64), cat("avp").reshape(1, 16, 512, 6, 64),
        cat("bkp").reshape(1, 16, S, 6, 64), cat("bvp").reshape(1, 16, S, 6, 64),
        cat("bip").reshape(1, 16, S, 32),
        cat("mkp").reshape(1, 16, 256, 4, 64), cat("mvp").reshape(1, 16, 256, 4, 64),
        stk("aks").reshape(1, 8, T, 6, 64), stk("avs").reshape(1, 8, T, 6, 64),
        stk("bks").reshape(1, 8, T, 6, 64), stk("bvs").reshape(1, 8, T, 6, 64),
        stk("bis").reshape(1, 8, T, 32),
    )
    return tuple(np.ascontiguousarray(o.astype(np.float32)) for o in out)
```
